# Optimizing a Trainium2 kernel written in Bass

```python
import math
import jax, jax.numpy as jnp
from jax import lax
import numpy as np

D_MODEL = 2048
BATCH = 8
SEQ = 4096
DEPTH = 1
DEC_BATCH = 4
DEC_SEQ = 8192
PAST_LEN = 128

HEAD_DIM = 64
N_HEADS_A = 16
N_KV_HEADS_A = 4
GROUP_A = N_HEADS_A // N_KV_HEADS_A
WINDOW = 128
BLOCK = 128
N_HEADS_B = 8
D_A = N_HEADS_A * HEAD_DIM
D_KV_A = N_KV_HEADS_A * HEAD_DIM
D_B = N_HEADS_B * 2 * HEAD_DIM
D_MIX = D_A + D_B
SPLIT_WIDTHS = [D_A, D_KV_A, D_KV_A, D_A, D_B, D_B, D_B, D_B]
D_IN_PROJ = sum(SPLIT_WIDTHS)
SPLIT_POINTS = [int(c) for c in np.cumsum(SPLIT_WIDTHS)[:-1]]
NUM_BUCKETS = 32
MAX_DISTANCE = 128
EPS = 1e-6

kernel_name = "hymba_swa_sink_diffattn_t5bias_encoder"


def rms_norm(x, g):
    xf = x.astype(jnp.float32)
    y = xf * lax.rsqrt(jnp.mean(xf * xf, axis=-1, keepdims=True) + EPS)
    return (y * g.astype(jnp.float32)).astype(x.dtype)


def t5_bucket(rel):
    half = NUM_BUCKETS // 2
    max_exact = half // 2
    bucket = jnp.where(rel > 0, half, 0)
    n = jnp.abs(rel)
    nf = jnp.maximum(n, 1).astype(jnp.float32)
    large = max_exact + (jnp.log(nf / max_exact) / math.log(MAX_DISTANCE / max_exact)
                         * (half - max_exact)).astype(jnp.int32)
    large = jnp.minimum(large, half - 1)
    return bucket + jnp.where(n < max_exact, n, large)


def window_gqa(q, k, v, sink, rel_bias_a):
    B, S = q.shape[0], q.shape[1]
    nb = S // BLOCK
    qb = q.reshape(B, nb, BLOCK, N_KV_HEADS_A, GROUP_A, HEAD_DIM)
    pad = ((0, 0), (BLOCK, BLOCK), (0, 0), (0, 0))
    kp = jnp.pad(k, pad).reshape(B, nb + 2, BLOCK, N_KV_HEADS_A, HEAD_DIM)
    vp = jnp.pad(v, pad).reshape(B, nb + 2, BLOCK, N_KV_HEADS_A, HEAD_DIM)
    kb = jnp.concatenate([kp[:, :-2], kp[:, 1:-1], kp[:, 2:]], axis=2)
    vb = jnp.concatenate([vp[:, :-2], vp[:, 1:-1], vp[:, 2:]], axis=2)
    s = jnp.einsum('bnqhgd,bnkhd->bnhgqk', qb, kb).astype(jnp.float32) * (HEAD_DIM ** -0.5)
    qi = jnp.arange(BLOCK)[:, None]
    kj = jnp.arange(3 * BLOCK)[None, :]
    rel = kj - BLOCK - qi
    bias = rel_bias_a[t5_bucket(rel)].astype(jnp.float32)
    bias = bias.transpose(2, 0, 1).reshape(N_KV_HEADS_A, GROUP_A, BLOCK, 3 * BLOCK)
    kpos = jnp.arange(nb)[:, None] * BLOCK - BLOCK + kj
    valid = ((jnp.abs(rel) <= WINDOW)[None]
             & ((kpos >= 0) & (kpos < S))[:, None, :])
    s = jnp.where(valid[None, :, None, None], s + bias, -jnp.inf)
    sink_l = jnp.broadcast_to(sink.astype(jnp.float32).reshape(1, 1, N_KV_HEADS_A, GROUP_A, 1, 1),
                              s.shape[:-1] + (1,))
    p = jax.nn.softmax(jnp.concatenate([s, sink_l], axis=-1), axis=-1)[..., :-1]
    o = jnp.einsum('bnhgqk,bnkhd->bnqhgd', p.astype(v.dtype), vb)
    return o.reshape(B, S, D_A)


def diff_attention(q, k, v, lam, lam_init, subln_g, rel_bias_b):
    B, S = q.shape[0], q.shape[1]
    nb = S // BLOCK
    qb = q.reshape(B, nb, BLOCK, N_HEADS_B, 2, HEAD_DIM).transpose(1, 0, 2, 3, 4, 5)
    kpos = jnp.arange(S)
    scale = HEAD_DIM ** -0.5

    def one_block(args):
        qblk, n = args
        s = jnp.einsum('bqhcd,bkhcd->bhcqk', qblk, k).astype(jnp.float32) * scale
        qpos = n * BLOCK + jnp.arange(BLOCK)
        rel = kpos[None, :] - qpos[:, None]
        bias = rel_bias_b[t5_bucket(rel)].astype(jnp.float32).transpose(2, 0, 1)
        p = jax.nn.softmax(s + bias[None, :, None], axis=-1)
        pd = p[:, :, 0] - lam * p[:, :, 1]
        return jnp.einsum('bhqk,bkhe->bqhe', pd.astype(v.dtype), v)

    o = lax.map(one_block, (qb, jnp.arange(nb)))
    o = o.transpose(1, 0, 2, 3, 4).reshape(B, S, N_HEADS_B, 2 * HEAD_DIM)
    o = rms_norm(o, subln_g) * (1.0 - lam_init)
    return o.reshape(B, S, D_B)


def mixer_layer(x, layer_idx, norm_g, w_in, w_out, sink, lq1, lk1, lq2, lk2, subln_g, rel_bias):
    B, S, _ = x.shape
    h = rms_norm(x, norm_g) @ w_in
    qa, ka, va, ga, qb, kb, vb, gb = jnp.split(h, SPLIT_POINTS, axis=-1)
    oa = window_gqa(qa.reshape(B, S, N_HEADS_A, HEAD_DIM),
                    ka.reshape(B, S, N_KV_HEADS_A, HEAD_DIM),
                    va.reshape(B, S, N_KV_HEADS_A, HEAD_DIM),
                    sink, rel_bias[:, :N_HEADS_A])
    lam_init = 0.8 - 0.6 * math.exp(-0.3 * layer_idx)
    lam = (jnp.exp(jnp.sum(lq1.astype(jnp.float32) * lk1.astype(jnp.float32)))
           - jnp.exp(jnp.sum(lq2.astype(jnp.float32) * lk2.astype(jnp.float32))) + lam_init)
    ob = diff_attention(qb.reshape(B, S, N_HEADS_B, 2, HEAD_DIM),
                        kb.reshape(B, S, N_HEADS_B, 2, HEAD_DIM),
                        vb.reshape(B, S, N_HEADS_B, 2 * HEAD_DIM),
                        lam, lam_init, subln_g, rel_bias[:, N_HEADS_A:])
    mixed = jnp.concatenate([jax.nn.silu(ga) * oa, jax.nn.silu(gb) * ob], axis=-1)
    return x + mixed @ w_out


def encode(x, norm_g, w_in, w_out, sink, lambda_q1, lambda_k1, lambda_q2, lambda_k2, subln_g,
           rel_bias, final_g):
    for l in range(DEPTH):
        x = mixer_layer(x, l, norm_g[l], w_in[l], w_out[l], sink[l], lambda_q1[l], lambda_k1[l],
                        lambda_q2[l], lambda_k2[l], subln_g[l], rel_bias)
    return rms_norm(x, final_g)


def setup_inputs(seed: int = 0) -> dict:
    key = jax.random.key(seed)
    ks = jax.random.split(key, 13)
    f32 = jnp.float32
    return {
        "x_prompt": jax.random.normal(ks[0], (BATCH, SEQ, D_MODEL), f32),
        "x_sample": jax.random.normal(ks[1], (DEC_BATCH, DEC_SEQ, D_MODEL), f32),
        "norm_g": 1.0 + 0.05 * jax.random.normal(ks[2], (DEPTH, D_MODEL), f32),
        "w_in": jax.random.normal(ks[3], (DEPTH, D_MODEL, D_IN_PROJ), f32) * D_MODEL ** -0.5,
        "w_out": jax.random.normal(ks[4], (DEPTH, D_MIX, D_MODEL), f32) * D_MIX ** -0.5,
        "sink": 0.5 * jax.random.normal(ks[5], (DEPTH, N_HEADS_A), f32),
        "lambda_q1": 0.1 * jax.random.normal(ks[6], (DEPTH, HEAD_DIM), f32),
        "lambda_k1": 0.1 * jax.random.normal(ks[7], (DEPTH, HEAD_DIM), f32),
        "lambda_q2": 0.1 * jax.random.normal(ks[8], (DEPTH, HEAD_DIM), f32),
        "lambda_k2": 0.1 * jax.random.normal(ks[9], (DEPTH, HEAD_DIM), f32),
        "subln_g": 1.0 + 0.05 * jax.random.normal(ks[10], (DEPTH, 2 * HEAD_DIM), f32),
        "rel_bias": 0.1 * jax.random.normal(ks[11], (NUM_BUCKETS, N_HEADS_A + N_HEADS_B), f32),
        "final_g": 1.0 + 0.05 * jax.random.normal(ks[12], (D_MODEL,), f32),
    }


def reference(x_prompt, x_sample, norm_g, w_in, w_out, sink, lambda_q1, lambda_k1, lambda_q2,
              lambda_k2, subln_g, rel_bias, final_g):
    y_prompt = encode(x_prompt, norm_g, w_in, w_out, sink, lambda_q1, lambda_k1, lambda_q2,
                      lambda_k2, subln_g, rel_bias, final_g)
    y_sample = encode(x_sample, norm_g, w_in, w_out, sink, lambda_q1, lambda_k1, lambda_q2,
                      lambda_k2, subln_g, rel_bias, final_g)
    return (y_prompt, y_sample)
```

```python
import math
from contextlib import ExitStack

import numpy as np
import ml_dtypes

import concourse.bass as bass
import concourse.mybir as mybir
from concourse.bass_utils import run_bass_kernel_spmd

F32 = mybir.dt.float32
BF16 = mybir.dt.bfloat16
AF = mybir.ActivationFunctionType
ALU = mybir.AluOpType
AX = mybir.AxisListType

D = 2048
DIN = 6656
EPS = 1e-6
NEG = -30000.0
LAM_INIT = 0.8 - 0.6 * math.exp(-0.3 * 0)
C_QA, C_KA, C_VA, C_GA, C_QB, C_KB, C_VB, C_GB = 0, 1024, 1280, 1536, 2560, 3584, 4608, 5632


class Stream:
    def __init__(self, k, name, uid):
        self.h = k.top.enter_context(k.nc.semaphore(name))
        self.n = 0
        self.name = name
        self.uid = uid


class KB:
    def __init__(self, nc, stack):
        self.nc = nc
        self.stack = stack
        self.top = stack
        self.by_name = {}
        self.engs = {"pe": nc.tensor, "act": nc.scalar, "dve": nc.vector, "pool": nc.gpsimd, "sp": nc.sync}
        self.streams = []
        self.waited = {}
        self.uid = 0

    def stream(self, name):
        if name in self.by_name:
            return self.by_name[name]
        s = Stream(self, name, len(self.streams))
        self.by_name[name] = s
        self.streams.append(s)
        return s

    def sig(self, instr, stream, inc=1):
        instr.then_inc(stream.h, inc)
        stream.n += inc
        return (stream, stream.n)

    def wait(self, eng, ev):
        if ev is None:
            return
        if isinstance(ev, list):
            for e in ev:
                self.wait(eng, e)
            return
        s, v = ev
        key = (eng, s.uid)
        if self.waited.get(key, 0) >= v:
            return
        self.engs[eng].wait_ge(s.h, v)
        self.waited[key] = v

    def barrier(self, final=False):
        for eng in self.engs:
            for s in self.streams:
                if s.n > 0 and (final or not getattr(s, "nobar", False)):
                    self.wait(eng, (s, s.n))

    def psum(self, name, shape, dt):
        self.uid += 1
        return self.stack.enter_context(self.nc.psum_tensor("%s_%d" % (name, self.uid), shape, dt))

    def sb(self, name, shape, dt):
        self.uid += 1
        return self.stack.enter_context(self.nc.sbuf_tensor("%s_%d" % (name, self.uid), shape, dt))


def t5_bucket_np(rel):
    rel = np.asarray(rel, dtype=np.int32)
    half, max_exact = 16, 8
    bucket = np.where(rel > 0, half, 0)
    n = np.abs(rel)
    nf = np.maximum(n, 1).astype(np.float32)
    large = max_exact + (np.log(nf / np.float32(max_exact)) / np.float32(math.log(128 / 8))
                         * np.float32(half - max_exact)).astype(np.int32)
    large = np.minimum(large, half - 1)
    return bucket + np.where(n < max_exact, n, large)


def host_consts():
    ident = np.eye(128, dtype=np.float32).astype(ml_dtypes.bfloat16)
    E = np.zeros((32, 512), np.float32)
    j = np.arange(511)
    E[t5_bucket_np(255 - j), j] = 1.0
    k = np.arange(128)[:, None]
    c = np.arange(384)[None, :]
    r = k + 128 - c
    mask = np.where(np.abs(r) <= 128, 0.0, NEG).astype(np.float32)
    return ident, E, mask


def build_program(TP, TS, debug=False):
    assert TP % 512 == 0 and TS % 1024 == 0
    nc = bass.Bass("TRN2", target_bir_lowering=False)
    TQS = TS // 2

    def din(name, shape, dt=F32):
        return nc.dram_tensor(name, shape, dt, kind="ExternalInput")

    xp_t = din("xp", [TP, D])
    xs_t = din("xs", [TS, D])
    w_in_t = din("w_in", [D, DIN])
    w_out_t = din("w_out", [D, D])
    norm_g_t = din("norm_g", [1, D])
    final_g_t = din("final_g", [1, D])
    sink_t = din("sink", [1, 16])
    lq1_t = din("lq1", [1, 64])
    lk1_t = din("lk1", [1, 64])
    lq2_t = din("lq2", [1, 64])
    lk2_t = din("lk2", [1, 64])
    subln_t = din("subln_g", [1, 128])
    rb_t = din("rb", [2, 32, 24])
    ident_t = din("ident", [128, 128], BF16)
    E_t = din("E", [32, 512])
    mask_t = din("mask", [128, 384])
    yp_t = nc.dram_tensor("yp", [TP, D], F32, kind="ExternalOutput")
    ys_t = nc.dram_tensor("ys", [TQS, D], F32, kind="ExternalOutput")

    Wb_t = nc.dram_tensor("Wb", [D, DIN], BF16)
    WOb_t = nc.dram_tensor("WOb", [D, D], BF16)
    Zd_t = nc.dram_tensor("Zd", [48, 128, 512], F32)
    dk = "ExternalOutput" if debug else "Internal"
    BTA_t = nc.dram_tensor("BTA", [2, 16, 128, 384], F32, kind=dk)
    BTB_t = nc.dram_tensor("BTB", [2, 8, 128, 2, 384], F32, kind=dk)

    jobs = []
    for ji, (x_t, y_t, T, TQ) in enumerate([(xp_t, yp_t, TP, TP), (xs_t, ys_t, TS, TQS)]):
        j = dict(idx=ji, x=x_t.ap(), y=y_t.ap(), T=T, TQ=TQ)
        j["QAT"] = nc.dram_tensor("QAT%d" % ji, [1024, TQ], BF16, kind=dk).ap()
        j["KAT"] = nc.dram_tensor("KAT%d" % ji, [256, T], BF16, kind=dk).ap()
        j["GAT"] = nc.dram_tensor("GAT%d" % ji, [1024, TQ], BF16, kind=dk).ap()
        j["QBT"] = nc.dram_tensor("QBT%d" % ji, [1024, TQ], BF16, kind=dk).ap()
        j["KBT"] = nc.dram_tensor("KBT%d" % ji, [1024, T], BF16, kind=dk).ap()
        j["GBT"] = nc.dram_tensor("GBT%d" % ji, [1024, TQ], BF16, kind=dk).ap()
        j["VA"] = nc.dram_tensor("VA%d" % ji, [T, 256], BF16, kind=dk).ap()
        j["VB"] = nc.dram_tensor("VB%d" % ji, [T, 1024], BF16, kind=dk).ap()
        j["MIXT"] = nc.dram_tensor("MIXT%d" % ji, [2048, TQ], BF16, kind=dk).ap()
        jobs.append(j)

    with ExitStack() as stack:
        K = KB(nc, stack)
        pe, act, dve, pool, sp = nc.tensor, nc.scalar, nc.vector, nc.gpsimd, nc.sync

        ident = K.sb("ident", [128, 128], BF16)
        gN = K.sb("gN", [128, D], F32)
        gF = K.sb("gF", [128, D], F32)
        ESK = K.sb("esk", [128, 16], F32)
        NEGLAM = K.sb("neglam", [128, 1], F32)
        SG = K.sb("sg", [128, 1], F32)
        FB = K.sb("fb", [128, 32], F32)
        ones_bf = K.sb("ones_bf", [128, 128], BF16)
        ones_f = K.sb("ones_f", [128, 128], F32)

        cast_ev = {}

        def bcast_rows(t, off, n):
            return bass.AP(t, off, [[0, 128], [1, n]])

        with ExitStack() as ps:
            K0 = K
            old_stack = K.stack
            K.stack = ps
            st_c = K.stream("cload")

            lam_t = [K.sb("lam%d" % i, [128, 64], F32) for i in range(4)]
            RB = K.sb("RB", [32, 48], F32)
            RBrep = K.sb("RBrep", [32, 48, 128], F32)
            Es = K.sb("Es", [32, 512], F32)
            MASK = K.sb("MASK", [128, 384], F32)
            sgt = K.sb("sgt", [128, 1], F32)
            lsum = K.sb("lsum", [128, 2], F32)
            lprod = K.sb("lprod", [128, 64], F32)
            lexp = K.sb("lexp", [128, 2], F32)

            K.sig(sp.dma_start(out=ident[:], in_=ident_t.ap()), st_c, 16)
            K.sig(sp.dma_start(out=gN[:], in_=bcast_rows(norm_g_t, 0, D)), st_c, 16)
            K.sig(sp.dma_start(out=gF[:], in_=bcast_rows(final_g_t, 0, D)), st_c, 16)
            K.sig(sp.dma_start(out=ESK[:], in_=bcast_rows(sink_t, 0, 16)), st_c, 16)
            for i, t in enumerate([lq1_t, lk1_t, lq2_t, lk2_t]):
                K.sig(sp.dma_start(out=lam_t[i][:], in_=bcast_rows(t, 0, 64)), st_c, 16)
            K.sig(sp.dma_start(out=sgt[:], in_=bass.AP(subln_t, 0, [[1, 128], [1, 1]])), st_c, 16)
            for tab in range(2):
                K.sig(sp.dma_start(out=RB[:, tab * 24:(tab + 1) * 24], in_=rb_t.ap()[tab]), st_c, 16)
            K.sig(sp.dma_start(out=Es[:], in_=E_t.ap()), st_c, 16)
            K.sig(sp.dma_start(out=MASK[:], in_=mask_t.ap()), st_c, 16)
            cl = (st_c, st_c.n)

            st_d = K.stream("p_dve")
            st_a = K.stream("p_act")
            st_p = K.stream("p_pe")
            K.wait("dve", cl)
            K.wait("act", cl)
            K.wait("pe", cl)
            dve.memset(ones_bf[:], 1.0)
            dve.memset(ones_f[:], 1.0)
            ev = K.sig(act.activation(out=ESK[:], in_=ESK[:], func=AF.Exp), st_a)
            e = K.sig(dve.tensor_tensor(out=lprod[:], in0=lam_t[0][:], in1=lam_t[1][:], op=ALU.mult), st_d)
            K.wait("dve", e)
            e = K.sig(dve.reduce_sum(out=lsum[:, 0:1], in_=lprod[:], axis=AX.X), st_d)
            K.wait("dve", e)
            e = K.sig(dve.tensor_tensor(out=lprod[:], in0=lam_t[2][:], in1=lam_t[3][:], op=ALU.mult), st_d)
            K.wait("dve", e)
            e = K.sig(dve.reduce_sum(out=lsum[:, 1:2], in_=lprod[:], axis=AX.X), st_d)
            K.wait("act", e)
            e = K.sig(act.activation(out=lexp[:], in_=lsum[:], func=AF.Exp), st_a)
            K.wait("dve", e)
            e = K.sig(dve.tensor_tensor(out=NEGLAM[:], in0=lexp[:, 1:2], in1=lexp[:, 0:1], op=ALU.subtract), st_d)
            K.wait("dve", e)
            e = K.sig(dve.tensor_scalar(out=NEGLAM[:], in0=NEGLAM[:], scalar1=-LAM_INIT, scalar2=None, op0=ALU.add), st_d)
            e = K.sig(dve.tensor_scalar(out=SG[:], in0=sgt[:], scalar1=(1.0 - LAM_INIT), scalar2=None, op0=ALU.mult), st_d)
            e = K.sig(dve.tensor_copy(out=RBrep[:], in_=bass.AP(RB, 0, [[48, 32], [1, 48], [0, 128]])), st_d)
            rbrep_ready = e

            NS = 4
            Zs = [K.sb("Zs", [128, 512], F32) for i in range(NS)]
            BTs = [K.sb("BTs", [128, 384], F32) for i in range(NS)]
            BTo = [K.sb("BTo", [128, 2, 384], F32) for i in range(NS)]
            zp = [K.psum("zp", [128, 512], F32) for i in range(2)]
            st_z = [K.stream("zst%d" % i) for i in range(NS)]
            st_b = [K.stream("bld%d" % i) for i in range(NS)]
            st_o = [K.stream("bst%d" % i) for i in range(NS)]
            zp_free = [None, None]
            zs_free = [None] * NS
            bts_free = [None] * NS
            bto_free = [None] * NS
            pst = {}
            K.wait("pe", rbrep_ready)

            def stage_a(idx):
                tab, h = idx // 24, idx % 24
                s, z = idx % NS, idx % 2
                K.wait("pe", zp_free[z])
                e_mm = K.sig(pe.matmul(zp[z][:], lhsT=RBrep[:, idx, :], rhs=Es[:], start=True, stop=True), st_p)
                K.wait("dve", e_mm)
                K.wait("dve", zs_free[s])
                e_z = K.sig(dve.tensor_copy(out=Zs[s][:], in_=zp[z][:]), st_d)
                zp_free[z] = e_z
                e_z2 = None
                if h >= 16:
                    K.wait("dve", e_z)
                    fo = (tab * 8 + (h - 16)) * 2
                    e_z1 = K.sig(dve.tensor_copy(out=FB[:, fo:fo + 1], in_=Zs[s][:, 510:511]), st_d)
                    e_z2 = K.sig(dve.tensor_copy(out=FB[:, fo + 1:fo + 2], in_=Zs[s][:, 0:1]), st_d)
                    e_z2 = [e_z1, e_z2]
                K.wait("sp", e_z)
                e_st = K.sig(sp.dma_start(out=Zd_t.ap()[idx], in_=Zs[s][:]), st_z[s], 16)
                zs_free[s] = e_st
                pst[idx] = (e_st, e_z2)

            def stage_b(idx):
                s = idx % NS
                e_st, e_z2 = pst[idx]
                K.wait("sp", e_st)
                K.wait("sp", bts_free[s])
                e_ld = K.sig(sp.dma_start(out=BTs[s][:],
                                          in_=bass.AP(Zd_t, idx * 128 * 512 + 127, [[511, 128], [1, 384]])),
                             st_b[s], 16)
                pst[idx] = (e_ld, e_z2)

            def stage_c(idx):
                tab, h = idx // 24, idx % 24
                s = idx % NS
                e_ld, e_z2 = pst.pop(idx)
                K.wait("dve", e_ld)
                K.wait("dve", bto_free[s])
                if h < 16:
                    e_o = K.sig(dve.tensor_tensor(out=BTo[s][:, 0, :], in0=BTs[s][:], in1=MASK[:], op=ALU.add), st_d)
                    bts_free[s] = e_o
                    K.wait("sp", e_o)
                    e_os = K.sig(sp.dma_start(out=BTA_t.ap()[tab, h], in_=BTo[s][:, 0, :]), st_o[s], 16)
                else:
                    K.wait("dve", e_z2)
                    fo = (tab * 8 + (h - 16)) * 2
                    e_o1 = K.sig(dve.tensor_scalar(out=BTo[s][:, 0, :], in0=BTs[s][:], scalar1=FB[:, fo:fo + 1], scalar2=8.0,
                                                   op0=ALU.subtract, op1=ALU.mult), st_d)
                    e_o = K.sig(dve.tensor_scalar(out=BTo[s][:, 1, :], in0=BTs[s][:], scalar1=FB[:, fo + 1:fo + 2],
                                                  scalar2=8.0, op0=ALU.subtract, op1=ALU.mult), st_d)
                    bts_free[s] = [e_o1, e_o]
                    K.wait("sp", e_o1)
                    K.wait("sp", e_o)
                    e_os = K.sig(sp.dma_start(out=BTB_t.ap()[tab, h - 16], in_=BTo[s][:]), st_o[s], 16)
                bto_free[s] = e_os
                return e_os

            last_os = None
            for it in range(48 + 2):
                if it < 48:
                    stage_a(it)
                if 0 <= it - 1 < 48:
                    stage_b(it - 1)
                if 0 <= it - 2 < 48:
                    last_os = stage_c(it - 2)
            K.wait("pool", last_os)
            for i in range(NS):
                K.wait("pool", (st_o[i], st_o[i].n))
            for blk in range(13):
                stc = K.stream("cast%d" % blk)
                stc.nobar = True
                cast_ev[blk] = K.sig(pool.dma_start(out=Wb_t.ap()[:, blk * 512:(blk + 1) * 512],
                                                    in_=w_in_t.ap()[:, blk * 512:(blk + 1) * 512]), stc, 16)
            stc = K.stream("castwo")
            stc.nobar = True
            cast_ev["wo"] = K.sig(pool.dma_start(out=WOb_t.ap(), in_=w_out_t.ap()), stc, 16)
            K.barrier()
            K.stack = old_stack

        def gemm1(job):
            T, TQ = job["T"], job["TQ"]
            x = job["x"]
            NG = T // 512
            Wv = Wb_t.ap().rearrange("(kc p) c -> p kc c", p=128)
            with ExitStack() as es:
                old = K.stack
                K.stack = es
                xt = [K.sb("xt", [128, D], F32) for _ in range(2)]
                junk = K.sb("junk", [128, D], BF16)
                xn = [K.sb("xn", [128, D], BF16) for _ in range(2)]
                xnT = [K.sb("xnT", [128, 16, 512], BF16) for _ in range(2)]
                wb = [K.sb("wb", [128, 16, 512], BF16) for _ in range(3)]
                NSTG = 4
                stg = [K.sb("stg", [128, 512], BF16) for _ in range(NSTG)]
                ssq = [K.sb("ssq", [128, 1], F32) for _ in range(2)]
                rstd = [K.sb("rstd", [128, 1], F32) for _ in range(2)]
                tp = [K.psum("tp", [128, 1024], BF16) for i in range(2)]
                NACC = 6
                acc = [K.psum("acc", [128, 512], F32) for i in range(NACC)]

                st_x = [K.stream("g1x%d" % i) for i in range(2)]
                st_w = [K.stream("g1w%d" % i) for i in range(3)]
                st_s = [K.stream("g1s%d" % i) for i in range(NSTG)]
                st_pe = K.stream("g1pe")
                st_act = K.stream("g1act")
                st_dve = K.stream("g1dve")

                xt_free = [[], []]
                ssq_free = [None, None]
                rstd_free = [None, None]
                xn_free = [None, None]
                tp_free = [None, None]
                xnT_ready = [[], []]
                xnT_free = [None, None]
                wb_free = [None] * 3
                acc_free = [None] * NACC
                stg_free = [None] * NSTG
                cnt = dict(acc=0, stg=0, wblk=0)
                fe_state = {}

                def fe_part1(tg, tt):
                    ti = tg * 4 + tt
                    s = ti % 2
                    K.wait("sp", xt_free[s])
                    e_ld = K.sig(sp.dma_start(out=xt[s][:], in_=x[ti * 128:(ti + 1) * 128, :]), st_x[s], 16)
                    K.wait("act", e_ld)
                    K.wait("act", ssq_free[s])
                    e_sq = K.sig(act.activation(out=junk[:], in_=xt[s][:], func=AF.Square, accum_out=ssq[s][:]), st_act)
                    K.wait("act", e_sq)
                    K.wait("act", rstd_free[s])
                    e1 = K.sig(act.activation(out=rstd[s][:], in_=ssq[s][:], func=AF.Sqrt, bias=EPS, scale=1.0 / D), st_act)
                    ssq_free[s] = e1
                    K.wait("dve", e1)
                    e2 = K.sig(dve.reciprocal(out=rstd[s][:], in_=rstd[s][:]), st_dve)
                    K.wait("dve", e2)
                    K.wait("dve", xn_free[s])
                    K.wait("dve", e_ld)
                    e3 = K.sig(dve.scalar_tensor_tensor(out=xn[s][:], in0=xt[s][:], scalar=rstd[s][:], in1=gN[:],
                                                        op0=ALU.mult, op1=ALU.mult), st_dve)
                    xt_free[s] = [e_sq, e3]
                    rstd_free[s] = e3
                    fe_state[ti] = e3

                def fe_part2(tg, tt):
                    ti = tg * 4 + tt
                    s = ti % 2
                    buf = tg % 2
                    K.wait("pe", fe_state.pop(ti))
                    evs = []
                    for b in range(2):
                        K.wait("pe", tp_free[b])
                        for jj in range(8):
                            kc = b * 8 + jj
                            ins = pe.transpose(tp[b][:, jj * 128:(jj + 1) * 128], xn[s][:, kc * 128:(kc + 1) * 128],
                                               ident[:])
                        e_t = K.sig(ins, st_pe)
                        if b == 1:
                            xn_free[s] = e_t
                        dst = xnT[buf][:, b * 8:(b + 1) * 8, tt * 128:(tt + 1) * 128]
                        src = tp[b][:].rearrange("p (j q) -> p j q", j=8)
                        if b == 0:
                            K.wait("act", e_t)
                            if tt == 0:
                                K.wait("act", xnT_free[buf])
                            e_c = K.sig(act.copy(out=dst, in_=src), st_act)
                        else:
                            K.wait("dve", e_t)
                            if tt == 0:
                                K.wait("dve", xnT_free[buf])
                            e_c = K.sig(dve.tensor_copy(out=dst, in_=src), st_dve)
                        tp_free[b] = e_c
                        evs.append(e_c)
                    if tt == 0:
                        xnT_ready[buf] = []
                    xnT_ready[buf] += evs

                def evac(a, kind, dst_ap, ncols=512, nparts=128):
                    s = cnt["stg"] % NSTG
                    cnt["stg"] += 1
                    eng = "act" if kind == "gate" else "dve"
                    K.wait(eng, acc_ready_ev[a])
                    K.wait(eng, stg_free[s])
                    if kind == "gate":
                        e = K.sig(act.activation(out=stg[s][:, 0:ncols], in_=acc[a][:, 0:ncols], func=AF.Silu), st_act)
                    else:
                        e = K.sig(dve.tensor_copy(out=stg[s][:, 0:ncols], in_=acc[a][:, 0:ncols]), st_dve)
                    acc_free[a] = e
                    K.wait("pool", e)
                    stg_free[s] = K.sig(pool.dma_start(out=dst_ap, in_=stg[s][:, 0:ncols]), st_s[s], 16)

                acc_ready_ev = [None] * NACC

                def blocks_for(full):
                    bl = []
                    if full:
                        bl.append((0, [("fm", "QAT", 0, c, "copy") for c in range(4)]))
                        bl.append((512, [("fm", "QAT", 512, c, "copy") for c in range(4)]))
                    bl.append((1024, [("fm", "KAT", 0, 0, "copy"), ("fm", "KAT", 0, 1, "copy"), ("tm", "VA", 0, 256, 256)]))
                    if full:
                        bl.append((1536, [("fm", "GAT", 0, c, "gate") for c in range(4)]))
                        bl.append((2048, [("fm", "GAT", 512, c, "gate") for c in range(4)]))
                        bl.append((2560, [("fm", "QBT", 0, c, "copy") for c in range(4)]))
                        bl.append((3072, [("fm", "QBT", 512, c, "copy") for c in range(4)]))
                    bl.append((3584, [("fm", "KBT", 0, c, "copy") for c in range(4)]))
                    bl.append((4096, [("fm", "KBT", 512, c, "copy") for c in range(4)]))
                    bl.append((4608, [("tm", "VB", 0, 0, 512)]))
                    bl.append((5120, [("tm", "VB", 512, 0, 512)]))
                    if full:
                        bl.append((5632, [("fm", "GBT", 0, c, "gate") for c in range(4)]))
                        bl.append((6144, [("fm", "GBT", 512, c, "gate") for c in range(4)]))
                    return bl

                def do_block(tg, col0, items):
                    buf = tg % 2
                    tok0 = tg * 512
                    ws = cnt["wblk"] % 3
                    cnt["wblk"] += 1
                    K.wait("sp", wb_free[ws])
                    K.wait("sp", cast_ev[col0 // 512])
                    e_w = K.sig(sp.dma_start(out=wb[ws][:], in_=Wv[:, :, col0:col0 + 512]), st_w[ws], 16)
                    K.wait("pe", e_w)
                    K.wait("pe", xnT_ready[buf])
                    last = None
                    for it in items:
                        if it[0] == "fm":
                            _, dname, rbase, c, kind = it
                            a = cnt["acc"] % NACC
                            cnt["acc"] += 1
                            K.wait("pe", acc_free[a])
                            for kc in range(16):
                                ins = pe.matmul(acc[a][:], lhsT=wb[ws][:, kc, c * 128:(c + 1) * 128],
                                                rhs=xnT[buf][:, kc, :], start=(kc == 0), stop=(kc == 15))
                            last = K.sig(ins, st_pe)
                            acc_ready_ev[a] = last
                            r0 = rbase + c * 128
                            evac(a, kind, job[dname][r0:r0 + 128, tok0:tok0 + 512])
                        else:
                            _, dname, dcol0, wc0, ncols = it
                            for tt in range(4):
                                a = cnt["acc"] % NACC
                                cnt["acc"] += 1
                                K.wait("pe", acc_free[a])
                                for kc in range(16):
                                    ins = pe.matmul(acc[a][:, 0:ncols], lhsT=xnT[buf][:, kc, tt * 128:(tt + 1) * 128],
                                                    rhs=wb[ws][:, kc, wc0:wc0 + ncols], start=(kc == 0), stop=(kc == 15))
                                last = K.sig(ins, st_pe)
                                acc_ready_ev[a] = last
                                t0 = tok0 + tt * 128
                                evac(a, "copy", job[dname][t0:t0 + 128, dcol0:dcol0 + ncols], ncols=ncols)
                    wb_free[ws] = last
                    return last

                for tt in range(4):
                    fe_part1(0, tt)
                    fe_part2(0, tt)
                for tg in range(NG):
                    full = (tg * 512) < TQ
                    bl = blocks_for(full)
                    last = None
                    for bi, (col0, items) in enumerate(bl):
                        last = do_block(tg, col0, items)
                        if tg + 1 < NG:
                            if bi < 4:
                                fe_part1(tg + 1, bi)
                            if 1 <= bi < 5:
                                fe_part2(tg + 1, bi - 1)
                    xnT_free[tg % 2] = last
                K.barrier()
                K.stack = old

        def attn_window(job, tab):
            T, TQ = job["T"], job["TQ"]
            NKB = T // 128
            NQB = TQ // 128
            QAv = job["QAT"].rearrange("(h d) t -> d h t", d=64)
            GAv = job["GAT"].rearrange("(h d) t -> d h t", d=64)
            MIXv = job["MIXT"][0:1024, :].rearrange("(h d) t -> d h t", d=64)
            VAv = job["VA"].rearrange("(kb p) c -> p kb c", p=128)
            with ExitStack() as es:
                old = K.stack
                K.stack = es
                HQ = TQ // 2
                NQH = NQB // 2
                KAw = [K.sb("KAw", [64, T], BF16) for _ in range(2)]
                VAw = [K.sb("VAw", [128, NKB, 64], BF16) for _ in range(2)]
                BAw = [K.sb("BAw", [128, 4, 384], F32) for _ in range(2)]
                QAw = [K.sb("QAw", [64, 4, HQ], BF16) for _ in range(2)]
                GAw = [K.sb("GAw", [64, 4, HQ], BF16) for _ in range(2)]
                NPA = 8
                PA = [K.sb("PA", [128, 512], BF16) for _ in range(NPA)]
                DENs = [K.sb("DEN", [64, 512], F32) for _ in range(2)]
                RRs = [K.sb("RR", [64, 512], F32) for _ in range(2)]
                OOs = [K.sb("OO", [64, 512], F32) for _ in range(2)]
                MSA = [K.sb("MSA", [64, 512], BF16) for _ in range(2)]
                NSA = 4
                SA = [K.psum("SA", [128, 512], F32) for i in range(NSA)]
                OAs = [K.psum("OA", [64, 512], F32) for i in range(2)]
                LAs = [K.psum("LA", [64, 512], F32) for i in range(2)]
                st_pool = K.stream("a1pool")
                st_kv = [K.stream("a1kv%d" % i) for i in range(2)]
                st_qg = [K.stream("a1qg%d" % i) for i in range(2)]
                st_ms = [K.stream("a1ms%d" % i) for i in range(2)]
                st_pe = K.stream("a1pe")
                st_act = K.stream("a1act")
                st_dve = K.stream("a1dve")
                sa_free = [None] * NSA
                pa_free = [None] * NPA
                ms_free = [None, None]
                cnt = dict(sa=0, ms=0)
                oa_free = [None, None]
                den_free = [None, None]
                rr_free = [None, None]
                oo_free = [None, None]
                cnt["it"] = 0
                cnt["pa"] = 0
                kv_free = [[], []]
                qg_free = [[], []]
                kv_ev = {}
                qg_ev = {}
                lastev = {}

                def load_kv(j):
                    hs = j % 2
                    K.wait("sp", kv_free[hs])
                    K.sig(sp.dma_start(out=KAw[hs][:], in_=job["KAT"][j * 64:(j + 1) * 64, :]), st_kv[hs], 16)
                    K.sig(sp.dma_start(out=VAw[hs][:], in_=VAv[:, :, j * 64:(j + 1) * 64]), st_kv[hs], 16)
                    for g in range(4):
                        e = K.sig(sp.dma_start(out=BAw[hs][:, g, :], in_=BTA_t.ap()[tab, 4 * j + g]), st_kv[hs], 16)
                    kv_ev[j] = e

                def load_qg(gi):
                    j, c = gi // 2, gi % 2
                    b = gi % 2
                    K.wait("sp", qg_free[b])
                    K.sig(sp.dma_start(out=QAw[b][:], in_=QAv[:, 4 * j:4 * j + 4, c * HQ:(c + 1) * HQ]), st_qg[b], 16)
                    qg_ev[gi] = K.sig(sp.dma_start(out=GAw[b][:], in_=GAv[:, 4 * j:4 * j + 4, c * HQ:(c + 1) * HQ]),
                                      st_qg[b], 16)

                load_kv(0)
                load_qg(0)
                for j in range(4):
                    hs = j % 2
                    load_qg(2 * j + 1)
                    if j + 1 < 4:
                        load_kv(j + 1)
                    K.wait("pe", kv_ev[j])
                    K.wait("dve", kv_ev[j])
                    esk_b = bass.AP(ESK, 4 * j, [[16, 64], [1, 4], [0, 128]])
                    pend = None
                    pend2 = None
                    for n in range(NQB + 2):
                        cur = None
                        if n < NQB:
                            kbs = [kb for kb in (n - 1, n, n + 1) if 0 <= kb < NKB]
                            pas = []
                            for kb in kbs:
                                delta = kb - n
                                s = cnt["sa"] % NSA
                                cnt["sa"] += 1
                                K.wait("pe", sa_free[s])
                                gi = 2 * j + n // NQH
                                nl = n % NQH
                                K.wait("pe", qg_ev[gi])
                                e_qk = K.sig(pe.matmul(SA[s][:], lhsT=KAw[hs][0:64, kb * 128:(kb + 1) * 128],
                                                       rhs=QAw[gi % 2][0:64, :, nl * 128:(nl + 1) * 128], start=True, stop=True),
                                             st_pe)
                                lastev[("qk", gi)] = e_qk
                                K.wait("dve", e_qk)
                                c0 = (1 - delta) * 128
                                e_b = K.sig(dve.scalar_tensor_tensor(
                                    out=SA[s][:].rearrange("p (g q) -> p g q", g=4),
                                    in0=SA[s][:].rearrange("p (g q) -> p g q", g=4), scalar=0.125,
                                    in1=BAw[hs][:, :, c0:c0 + 128], op0=ALU.mult, op1=ALU.add), st_dve)
                                lastev[("bias", j)] = e_b
                                pa = cnt["pa"] % NPA
                                cnt["pa"] += 1
                                K.wait("act", e_b)
                                K.wait("act", pa_free[pa])
                                e_x = K.sig(act.activation(out=PA[pa][:], in_=SA[s][:], func=AF.Exp), st_act)
                                sa_free[s] = e_x
                                pas.append((pa, kb, e_x))
                            cur = (n, pas)
                        if pend2 is not None:
                            pn2, b2, e2, e_pv2 = pend2
                            OA, RR, OO = OAs[b2], RRs[b2], OOs[b2]
                            K.wait("dve", e2)
                            K.wait("dve", oo_free[b2])
                            e3 = K.sig(dve.tensor_tensor(out=OO[:], in0=OA[:], in1=RR[:], op=ALU.mult), st_dve)
                            oa_free[b2] = e3
                            rr_free[b2] = e3
                            m = cnt["ms"] % 2
                            cnt["ms"] += 1
                            K.wait("pool", e3)
                            K.wait("pool", ms_free[m])
                            gi2 = 2 * j + pn2 // NQH
                            nl2 = pn2 % NQH
                            K.wait("pool", qg_ev[gi2])
                            e4 = K.sig(pool.tensor_tensor(out=MSA[m][:].rearrange("p (g q) -> p g q", g=4),
                                                          in0=OO[:].rearrange("p (g q) -> p g q", g=4),
                                                          in1=GAw[gi2 % 2][:, :, nl2 * 128:(nl2 + 1) * 128], op=ALU.mult),
                                       st_pool)
                            lastev[("g", gi2)] = e4
                            if nl2 == NQH - 1:
                                qg_free[gi2 % 2] = [lastev[("qk", gi2)], e4]
                                if gi2 % 2 == 0 and gi2 + 2 < 8:
                                    load_qg(gi2 + 2)
                            oo_free[b2] = e4
                            K.wait("pool", e4)
                            ms_free[m] = K.sig(pool.dma_start(out=MIXv[:, 4 * j:4 * j + 4, pn2 * 128:(pn2 + 1) * 128],
                                                              in_=MSA[m][:].rearrange("p (g q) -> p g q", g=4)),
                                               st_ms[m], 16)
                            lastev[("pv", j)] = e_pv2
                            pend2 = None
                        if pend is not None:
                            pn, pas = pend
                            b2 = cnt["it"] % 2
                            cnt["it"] += 1
                            OA, LA, DEN, RR = OAs[b2], LAs[b2], DENs[b2], RRs[b2]
                            K.wait("pe", oa_free[b2])
                            for i, (s, kb, e_x) in enumerate(pas):
                                K.wait("pe", e_x)
                                pe.matmul(OA[:], lhsT=VAw[hs][:, kb, :], rhs=PA[s][:], start=(i == 0), stop=(i == len(pas) - 1))
                            for i, (s, kb, e_x) in enumerate(pas):
                                ins = pe.matmul(LA[:], lhsT=ones_bf[:, 0:64], rhs=PA[s][:], start=(i == 0),
                                                stop=(i == len(pas) - 1))
                            e_pv = K.sig(ins, st_pe)
                            for (s, kb, e_x) in pas:
                                pa_free[s] = e_pv
                            K.wait("dve", e_pv)
                            K.wait("dve", den_free[b2])
                            e1 = K.sig(dve.tensor_tensor(out=DEN[:].rearrange("p (g q) -> p g q", g=4),
                                                         in0=LA[:].rearrange("p (g q) -> p g q", g=4),
                                                         in1=esk_b, op=ALU.add), st_dve)
                            K.wait("act", e1)
                            K.wait("act", rr_free[b2])
                            e2 = K.sig(act.activation(out=DEN[:], in_=DEN[:], func=AF.Ln), st_act)
                            K.wait("act", e2)
                            e2 = K.sig(act.activation(out=RR[:], in_=DEN[:], func=AF.Exp, scale=-1.0), st_act)
                            den_free[b2] = e2
                            pend2 = (pn, b2, e2, e_pv)
                        pend = cur
                    kv_free[hs] = [lastev[("pv", j)], lastev[("bias", j)]]
                K.barrier()
                K.stack = old

        def attn_diff(job, tab):
            T, TQ = job["T"], job["TQ"]
            NKB = T // 128
            NG = TQ // 512
            VBv = job["VB"].rearrange("(kb p) c -> p kb c", p=128)
            with ExitStack() as es:
                old = K.stack
                K.stack = es
                KT = [K.sb("KT", [128, T], BF16) for _ in range(2)]
                VV = [K.sb("VV", [128, NKB, 128], BF16) for _ in range(2)]
                QT = [K.sb("QT", [128, TQ], BF16) for _ in range(2)]
                GT = [K.sb("GT", [128, TQ], BF16) for _ in range(2)]
                BP = [K.sb("BP", [128, 2, 384], F32) for _ in range(2)]
                NPT = 4
                PT = [K.sb("PT", [128, 1024], BF16) for _ in range(NPT)]
                OS = K.sb("OS", [128, 1024], F32)
                LS = K.sb("LS", [128, 1024], F32)
                LN = K.sb("LN", [128, 1024], F32)
                RR = K.sb("RR", [128, 1024], F32)
                Asb = K.sb("Asb", [128, 512], F32)
                Bsb = K.sb("Bsb", [128, 512], F32)
                Osb = K.sb("Osb", [128, 512], F32)
                SQ = K.sb("SQ", [128, 512], F32)
                RS = K.sb("RS", [128, 512], F32)
                RS2 = K.sb("RS2", [128, 512], F32)
                TT = K.sb("TT", [128, 512], F32)
                MS = [K.sb("MS", [128, 512], BF16) for _ in range(2)]
                ST = [K.psum("ST", [128, 1024], F32) for i in range(2)]
                OO = K.psum("OO", [128, 1024], F32)
                LL = K.psum("LL", [128, 1024], F32)
                st_ld = [K.stream("a2ld%d" % i) for i in range(2)]
                st_ms = [K.stream("a2ms%d" % i) for i in range(2)]
                st_pe = K.stream("a2pe")
                st_act = K.stream("a2act")
                st_dve = K.stream("a2dve")
                st_pool = K.stream("a2pool")

                slot_free = [[], []]
                st_cons = [None, None]
                pt_free = [None] * NPT
                ms_free = [None, None]
                cnt = dict(pt=0, ms=0, user=0)
                state = {}

                def load_head(h):
                    hs = h % 2
                    K.wait("sp", slot_free[hs])
                    K.sig(sp.dma_start(out=KT[hs][:], in_=job["KBT"][h * 128:(h + 1) * 128, :]), st_ld[hs], 16)
                    K.sig(sp.dma_start(out=QT[hs][:], in_=job["QBT"][h * 128:(h + 1) * 128, :]), st_ld[hs], 16)
                    K.sig(sp.dma_start(out=GT[hs][:], in_=job["GBT"][h * 128:(h + 1) * 128, :]), st_ld[hs], 16)
                    K.sig(sp.dma_start(out=BP[hs][:], in_=BTB_t.ap()[tab, h]), st_ld[hs], 16)
                    return K.sig(sp.dma_start(out=VV[hs][:], in_=VBv[:, :, h * 128:(h + 1) * 128]), st_ld[hs], 16)

                ld_ev = {0: load_head(0)}
                if True:
                    ld_ev[1] = load_head(1)
                    users = []
                    LB_AT = min(5, NKB - 2)
                    SS_AT = min(14, NKB - 1)
                    prev = None
                    for h in range(8):
                        for g in range(NG):
                            for kb in range(NKB):
                                users.append(("qk", (h, g), kb))
                                if prev is not None and kb == LB_AT:
                                    users.append(("lb", prev, None))
                                if prev is not None and kb == SS_AT:
                                    users.append(("ss", prev, None))
                            prev = (h, g)
                    users.append(("lb", prev, None))
                    users.append(("ss", prev, None))
                    info = {}
                    info2 = {}
                    acc_done = {}
                    sq_ready = {}

                    def deps_exist(ui):
                        kind, hg, kb = users[ui]
                        if kind == "lb":
                            return hg in acc_done
                        if kind == "ss":
                            return hg in sq_ready
                        return True

                    def produce(ui):
                        kind, hg, kb = users[ui]
                        h, g = hg
                        hs = h % 2
                        s = cnt["user"] % 2
                        cnt["user"] += 1
                        K.wait("pe", st_cons[s])
                        if kind == "qk":
                            K.wait("pe", ld_ev[h])
                            pe.matmul(ST[s][:, 0:512], lhsT=KT[hs][0:64, kb * 128:(kb + 1) * 128],
                                      rhs=QT[hs][0:64, g * 512:(g + 1) * 512], start=True, stop=True)
                            e = K.sig(pe.matmul(ST[s][:, 512:1024], lhsT=KT[hs][64:128, kb * 128:(kb + 1) * 128],
                                                rhs=QT[hs][64:128, g * 512:(g + 1) * 512], start=True, stop=True), st_pe)
                            near = [jq for jq in range(4) if abs(kb - (4 * g + jq)) <= 1]
                            if near:
                                side = 0 if kb <= 4 * g + 1 else 1
                                jl, jh = near[0], near[-1]
                                nn = jh - jl + 1
                                c0 = (1 - (kb - (4 * g + jl))) * 128
                                K.wait("dve", e)
                                K.wait("dve", ld_ev[h])
                                stv = ST[s][:].rearrange("p (m q) -> p m q", m=2)[:, :, jl * 128:(jh + 1) * 128]
                                bpv = bass.AP(BP[hs], side * 384 + c0, [[768, 128], [0, 2], [1, nn * 128]])
                                e = K.sig(dve.tensor_tensor(out=stv, in0=stv, in1=bpv, op=ALU.add), st_dve)
                        elif kind == "lb":
                            K.wait("pe", acc_done[hg])
                            e = K.sig(pe.matmul(ST[s][:, 512:1024], lhsT=ones_f[:], rhs=LS[:, 512:1024], start=True, stop=True),
                                      st_pe)
                            state["ls_pe"] = e
                        else:
                            K.wait("pe", sq_ready[hg])
                            e = K.sig(pe.matmul(ST[s][:, 0:512], lhsT=ones_f[:], rhs=SQ[:], start=True, stop=True), st_pe)
                        info[ui] = (s, e)

                    def consume(ui):
                        kind, hg, kb = users[ui]
                        h, g = hg
                        hs = h % 2
                        s, e_st = info.pop(ui)
                        if kind == "qk":
                            near = [jq for jq in range(4) if abs(kb - (4 * g + jq)) <= 1]
                            side = 0 if kb <= 4 * g + 1 else 1
                            if not near:
                                side = 1 if kb > 4 * g + 3 else 0
                            e_in = e_st
                            p = cnt["pt"] % NPT
                            cnt["pt"] += 1
                            K.wait("act", e_in)
                            K.wait("act", pt_free[p])
                            fo = (tab * 8 + h) * 2 + side
                            e_x = K.sig(act.activation(out=PT[p][:], in_=ST[s][:], func=AF.Exp,
                                                       bias=FB[:, fo:fo + 1], scale=0.125), st_act)
                            st_cons[s] = e_x
                            info2[ui] = (p, e_x)
                        elif kind == "lb":
                            K.wait("act", e_st)
                            K.wait("act", state.get("rr_free"))
                            e = K.sig(act.activation(out=LN[:, 512:1024], in_=ST[s][:, 512:1024], func=AF.Ln), st_act)
                            st_cons[s] = e
                            K.wait("act", acc_done[hg])
                            e = K.sig(act.activation(out=LN[:, 0:512], in_=LS[:, 0:512], func=AF.Ln), st_act)
                            state["ls_free"] = e
                            K.wait("act", e)
                            e = K.sig(act.activation(out=RR[:], in_=LN[:], func=AF.Exp, scale=-1.0), st_act)
                            K.wait("dve", e)
                            K.wait("dve", state.get("ab_free"))
                            K.wait("dve", state.get("os_ready"))
                            e = K.sig(dve.tensor_tensor(out=Asb[:], in0=OS[:, 0:512], in1=RR[:, 0:512], op=ALU.mult), st_dve)
                            e = K.sig(dve.scalar_tensor_tensor(out=Bsb[:], in0=OS[:, 512:1024], scalar=NEGLAM[:],
                                                               in1=RR[:, 512:1024], op0=ALU.mult, op1=ALU.mult), st_dve)
                            state["os_free"] = e
                            state["rr_free"] = e
                            K.wait("pool", e)
                            K.wait("pool", state.get("osb_free"))
                            e = K.sig(pool.tensor_tensor(out=Osb[:], in0=Asb[:], in1=Bsb[:], op=ALU.add), st_pool)
                            state["ab_free"] = e
                            K.wait("pool", e)
                            K.wait("pool", state.get("sq_free"))
                            e = K.sig(pool.tensor_tensor(out=SQ[:], in0=Osb[:], in1=Osb[:], op=ALU.mult), st_pool)
                            sq_ready[hg] = e
                        else:
                            state["sq_free"] = e_st
                            K.wait("act", e_st)
                            K.wait("act", state.get("rs_free"))
                            e = K.sig(act.activation(out=RS[:], in_=ST[s][:, 0:512], func=AF.Ln, bias=EPS, scale=1.0 / 128),
                                      st_act)
                            st_cons[s] = e
                            K.wait("act", e)
                            e = K.sig(act.activation(out=RS2[:], in_=RS[:], func=AF.Exp, scale=-0.5), st_act)
                            K.wait("pool", e)
                            K.wait("pool", state.get("tt_free"))
                            e = K.sig(pool.tensor_tensor(out=TT[:], in0=Osb[:], in1=RS2[:], op=ALU.mult), st_pool)
                            state["osb_free"] = e
                            state["rs_free"] = e
                            K.wait("dve", e)
                            m = cnt["ms"] % 2
                            cnt["ms"] += 1
                            K.wait("dve", ms_free[m])
                            K.wait("dve", ld_ev[h])
                            e = K.sig(dve.scalar_tensor_tensor(out=MS[m][:], in0=TT[:], scalar=SG[:],
                                                               in1=GT[hs][:, g * 512:(g + 1) * 512],
                                                               op0=ALU.mult, op1=ALU.mult), st_dve)
                            state["tt_free"] = e
                            K.wait("pool", e)
                            r0 = 1024 + h * 128
                            ms_free[m] = K.sig(pool.dma_start(out=job["MIXT"][r0:r0 + 128, g * 512:(g + 1) * 512],
                                                              in_=MS[m][:]), st_ms[m], 16)
                            state["last"] = e
                            if g == NG - 1 and h + 2 < 8:
                                slot_free[hs] = state[("hd", h)] + [e]
                                ld_ev[h + 2] = load_head(h + 2)


                    def consume_b(ui):
                        kind, hg, kb = users[ui]
                        h, g = hg
                        hs = h % 2
                        if kind != "qk":
                            return
                        p, e_x = info2.pop(ui)
                        if True:
                            first, last = (kb == 0), (kb == NKB - 1)
                            K.wait("pe", e_x)
                            if first:
                                K.wait("pe", state.get("oo_free"))
                            pe.matmul(OO[:, 0:512], lhsT=VV[hs][:, kb, :], rhs=PT[p][:, 0:512], start=first, stop=last)
                            pe.matmul(OO[:, 512:1024], lhsT=VV[hs][:, kb, :], rhs=PT[p][:, 512:1024], start=first, stop=last)
                            e_pv = K.sig(pe.matmul(LL[:, 0:512], lhsT=ones_bf[:], rhs=PT[p][:, 0:512], start=first, stop=last),
                                         st_pe)
                            K.wait("dve", e_x)
                            if first:
                                e_d = K.sig(dve.tensor_copy(out=LL[:, 512:1024], in_=PT[p][:, 512:1024]), st_dve)
                            else:
                                e_d = K.sig(dve.tensor_tensor(out=LL[:, 512:1024], in0=LL[:, 512:1024], in1=PT[p][:, 512:1024],
                                                              op=ALU.add), st_dve)
                            pt_free[p] = [e_pv, e_d]
                            if last:
                                K.wait("dve", e_pv)
                                K.wait("dve", state.get("os_free"))
                                K.wait("dve", state.get("ls_free"))
                                K.wait("dve", state.get("ls_pe"))
                                e_os = K.sig(dve.tensor_copy(out=OS[:], in_=OO[:]), st_dve)
                                state["os_ready"] = e_os
                                e = K.sig(dve.tensor_copy(out=LS[:, 0:512], in_=LL[:, 0:512]), st_dve)
                                state["oo_free"] = [e_os, e]
                                e = K.sig(dve.tensor_copy(out=LS[:, 512:1024], in_=LL[:, 512:1024]), st_dve)
                                acc_done[hg] = e
                                state[("hd", h)] = [e_pv, e_d]

                    NU = len(users)
                    nxt = 0
                    while nxt <= min(1, NU - 1) and deps_exist(nxt):
                        produce(nxt)
                        nxt += 1
                    deferred = None
                    for ui in range(NU):
                        assert nxt > ui
                        consume(ui)
                        if users[ui][0] == "qk":
                            while nxt <= min(ui + 2, NU - 1) and deps_exist(nxt):
                                produce(nxt)
                                nxt += 1
                        if deferred is not None:
                            consume_b(deferred)
                            deferred = None
                        if users[ui][0] == "qk" and users[ui][2] == 0 and ui + 1 < NU and users[ui + 1][0] == "qk":
                            deferred = ui
                        else:
                            consume_b(ui)
                        while nxt <= min(ui + 2, NU - 1) and deps_exist(nxt):
                            produce(nxt)
                            nxt += 1
                    assert deferred is None
                K.barrier()
                K.stack = old

        def gemm3(job):
            TQ = job["TQ"]
            x, y = job["x"], job["y"]
            NG = TQ // 512
            WOv = WOb_t.ap().rearrange("(kc p) c -> p kc c", p=128)
            MTv = job["MIXT"].rearrange("(kc p) t -> p kc t", p=128)
            with ExitStack() as es:
                old = K.stack
                K.stack = es
                WO = K.sb("WO", [128, 16, D], BF16)
                MT = [K.sb("MT", [128, 16, 512], BF16) for _ in range(2)]
                xt = [K.sb("xt3", [128, D], F32) for _ in range(2)]
                zt = [K.sb("zt", [128, D], F32) for _ in range(2)]
                ot = [K.sb("ot", [128, D], F32) for _ in range(2)]
                junk = K.sb("junk3", [128, D], BF16)
                ssq = [K.sb("ssq3", [128, 1], F32) for _ in range(2)]
                rstd = [K.sb("rstd3", [128, 1], F32) for _ in range(2)]
                Y = [K.psum("Y", [128, D], F32) for i in range(2)]
                st_wo = K.stream("g3wo")
                st_mt = [K.stream("g3mt%d" % i) for i in range(2)]
                st_x = [K.stream("g3x%d" % i) for i in range(2)]
                st_o = [K.stream("g3o%d" % i) for i in range(2)]
                st_pe = K.stream("g3pe")
                st_act = K.stream("g3act")
                st_dve = K.stream("g3dve")
                K.wait("sp", cast_ev["wo"])
                for q in range(4):
                    e_wo = K.sig(sp.dma_start(out=WO[:, q * 4:(q + 1) * 4, :], in_=WOv[:, q * 4:(q + 1) * 4, :]), st_wo, 16)
                K.wait("pe", e_wo)
                mt_free = [None, None]
                xt_free = [None, None]
                y_free = [None, None]
                zt_free = [None, None]
                ot_free = [None, None]
                ssq_free = [None, None]
                rstd_free = [None, None]
                mt_ev = {}

                def load_mt(tg):
                    b = tg % 2
                    K.wait("sp", mt_free[b])
                    mt_ev[tg] = K.sig(sp.dma_start(out=MT[b][:], in_=MTv[:, :, tg * 512:(tg + 1) * 512]), st_mt[b], 16)

                load_mt(0)
                for tg in range(NG):
                    b = tg % 2
                    if tg + 1 < NG:
                        load_mt(tg + 1)
                    K.wait("pe", mt_ev[tg])
                    for tt in range(4):
                        ti = tg * 4 + tt
                        s = ti % 2
                        K.wait("sp", xt_free[s])
                        e_x = K.sig(sp.dma_start(out=xt[s][:], in_=x[ti * 128:(ti + 1) * 128, :]), st_x[s], 16)
                        K.wait("pe", y_free[s])
                        for cb in range(4):
                            for kc in range(16):
                                ins = pe.matmul(Y[s][:, cb * 512:(cb + 1) * 512], lhsT=MT[b][:, kc, tt * 128:(tt + 1) * 128],
                                                rhs=WO[:, kc, cb * 512:(cb + 1) * 512], start=(kc == 0), stop=(kc == 15))
                        e_y = K.sig(ins, st_pe)
                        if tt == 3:
                            mt_free[b] = e_y
                        K.wait("dve", e_y)
                        K.wait("dve", e_x)
                        K.wait("dve", zt_free[s])
                        e_z = K.sig(dve.tensor_tensor(out=zt[s][:], in0=Y[s][:], in1=xt[s][:], op=ALU.add), st_dve)
                        y_free[s] = e_z
                        xt_free[s] = e_z
                        K.wait("act", e_z)
                        K.wait("act", ssq_free[s])
                        e_sq = K.sig(act.activation(out=junk[:], in_=zt[s][:], func=AF.Square, accum_out=ssq[s][:]), st_act)
                        K.wait("act", e_sq)
                        K.wait("act", rstd_free[s])
                        e1 = K.sig(act.activation(out=rstd[s][:], in_=ssq[s][:], func=AF.Sqrt, bias=EPS, scale=1.0 / D),
                                   st_act)
                        ssq_free[s] = e1
                        K.wait("dve", e1)
                        e2 = K.sig(dve.reciprocal(out=rstd[s][:], in_=rstd[s][:]), st_dve)
                        K.wait("dve", e2)
                        K.wait("dve", ot_free[s])
                        e3 = K.sig(dve.scalar_tensor_tensor(out=ot[s][:], in0=zt[s][:], scalar=rstd[s][:], in1=gF[:],
                                                            op0=ALU.mult, op1=ALU.mult), st_dve)
                        zt_free[s] = e3
                        rstd_free[s] = e3
                        K.wait("pool", e3)
                        ot_free[s] = K.sig(pool.dma_start(out=y[ti * 128:(ti + 1) * 128, :], in_=ot[s][:]), st_o[s], 16)
                K.barrier()
                K.stack = old

        for ji, job in enumerate(jobs):
            gemm1(job)
            attn_window(job, ji)
            attn_diff(job, ji)
            gemm3(job)
        K.barrier(final=True)
    return nc


def make_in_maps(inputs, TP, TS, n_cores=8):
    ident, E, mask = host_consts()
    rb = np.asarray(inputs["rel_bias"], np.float32)
    rb_sw = rb.copy()
    rb_sw[1:16] = rb[17:32]
    rb_sw[17:32] = rb[1:16]
    w_in = np.ascontiguousarray(np.asarray(inputs["w_in"], np.float32)[0])
    w_out = np.ascontiguousarray(np.asarray(inputs["w_out"], np.float32)[0])
    maps = []
    for c in range(n_cores):
        xp = np.ascontiguousarray(np.asarray(inputs["x_prompt"][c], np.float32)[:TP])
        xs = np.asarray(inputs["x_sample"][c // 2], np.float32)[:TS]
        if c % 2 == 1:
            xs = xs[::-1]
        xs = np.ascontiguousarray(xs)
        rbc = np.stack([rb, rb if c % 2 == 0 else rb_sw]).astype(np.float32)
        maps.append({
            "xp": xp, "xs": xs, "w_in": w_in, "w_out": w_out,
            "norm_g": np.asarray(inputs["norm_g"], np.float32).reshape(1, D),
            "final_g": np.asarray(inputs["final_g"], np.float32).reshape(1, D),
            "sink": np.asarray(inputs["sink"], np.float32).reshape(1, 16),
            "lq1": np.asarray(inputs["lambda_q1"], np.float32).reshape(1, 64),
            "lk1": np.asarray(inputs["lambda_k1"], np.float32).reshape(1, 64),
            "lq2": np.asarray(inputs["lambda_q2"], np.float32).reshape(1, 64),
            "lk2": np.asarray(inputs["lambda_k2"], np.float32).reshape(1, 64),
            "subln_g": np.asarray(inputs["subln_g"], np.float32).reshape(1, 128),
            "rb": rbc, "ident": ident, "E": E, "mask": mask,
        })
    return maps


def kernel(x_prompt, x_sample, norm_g, w_in, w_out, sink, lambda_q1, lambda_k1, lambda_q2, lambda_k2,
           subln_g, rel_bias, final_g):
    TP, TS = 4096, 8192
    inputs = dict(x_prompt=x_prompt, x_sample=x_sample, norm_g=norm_g, w_in=w_in, w_out=w_out, sink=sink,
                  lambda_q1=lambda_q1, lambda_k1=lambda_k1, lambda_q2=lambda_q2, lambda_k2=lambda_k2,
                  subln_g=subln_g, rel_bias=rel_bias, final_g=final_g)
    nc = build_program(TP, TS)
    maps = make_in_maps(inputs, TP, TS)
    res = run_bass_kernel_spmd(nc, maps, core_ids=list(range(8)))
    y_prompt = np.empty((8, TP, D), np.float32)
    y_sample = np.empty((4, TS, D), np.float32)
    for c in range(8):
        r = res.results[c]
        y_prompt[c] = r["yp"]
        if c % 2 == 0:
            y_sample[c // 2, :TS // 2] = r["ys"]
        else:
            y_sample[c // 2, TS // 2:] = r["ys"][::-1]
    return (y_prompt, y_sample)
```

```python
import math
from contextlib import ExitStack

import numpy as np
import ml_dtypes

import concourse.bass as bass
import concourse.mybir as mybir
from concourse.bass_utils import run_bass_kernel_spmd

F32 = mybir.dt.float32
BF16 = mybir.dt.bfloat16
AF = mybir.ActivationFunctionType
ALU = mybir.AluOpType
AX = mybir.AxisListType

D = 2048
DIN = 6656
EPS = 1e-6
NEG = -30000.0
LAM_INIT = 0.8 - 0.6 * math.exp(-0.3 * 0)
C_QA, C_KA, C_VA, C_GA, C_QB, C_KB, C_VB, C_GB = 0, 1024, 1280, 1536, 2560, 3584, 4608, 5632


class Stream:
    def __init__(self, k, name, uid):
        self.h = k.top.enter_context(k.nc.semaphore(name))
        self.n = 0
        self.name = name
        self.uid = uid


class KB:
    def __init__(self, nc, stack):
        self.nc = nc
        self.stack = stack
        self.top = stack
        self.by_name = {}
        self.engs = {"pe": nc.tensor, "act": nc.scalar, "dve": nc.vector, "pool": nc.gpsimd, "sp": nc.sync}
        self.streams = []
        self.waited = {}
        self.uid = 0

    def stream(self, name):
        if name in self.by_name:
            return self.by_name[name]
        s = Stream(self, name, len(self.streams))
        self.by_name[name] = s
        self.streams.append(s)
        return s

    def sig(self, instr, stream, inc=1):
        instr.then_inc(stream.h, inc)
        stream.n += inc
        return (stream, stream.n)

    def wait(self, eng, ev):
        if ev is None:
            return
        if isinstance(ev, list):
            for e in ev:
                self.wait(eng, e)
            return
        s, v = ev
        key = (eng, s.uid)
        if self.waited.get(key, 0) >= v:
            return
        self.engs[eng].wait_ge(s.h, v)
        self.waited[key] = v

    def barrier(self, final=False):
        for eng in self.engs:
            for s in self.streams:
                if s.n > 0 and (final or not getattr(s, "nobar", False)):
                    self.wait(eng, (s, s.n))

    def psum(self, name, shape, dt):
        self.uid += 1
        return self.stack.enter_context(self.nc.psum_tensor("%s_%d" % (name, self.uid), shape, dt))

    def sb(self, name, shape, dt):
        self.uid += 1
        return self.stack.enter_context(self.nc.sbuf_tensor("%s_%d" % (name, self.uid), shape, dt))


def t5_bucket_np(rel):
    rel = np.asarray(rel, dtype=np.int32)
    half, max_exact = 16, 8
    bucket = np.where(rel > 0, half, 0)
    n = np.abs(rel)
    nf = np.maximum(n, 1).astype(np.float32)
    large = max_exact + (np.log(nf / np.float32(max_exact)) / np.float32(math.log(128 / 8))
                         * np.float32(half - max_exact)).astype(np.int32)
    large = np.minimum(large, half - 1)
    return bucket + np.where(n < max_exact, n, large)


def host_consts():
    ident = np.eye(128, dtype=np.float32).astype(ml_dtypes.bfloat16)
    E = np.zeros((32, 512), np.float32)
    j = np.arange(511)
    E[t5_bucket_np(255 - j), j] = 1.0
    k = np.arange(128)[:, None]
    c = np.arange(384)[None, :]
    r = k + 128 - c
    mask = np.where(np.abs(r) <= 128, 0.0, NEG).astype(np.float32)
    return ident, E, mask


def build_program(TP, TS, debug=False):
    assert TP % 512 == 0 and TS % 1024 == 0
    nc = bass.Bass("TRN2", target_bir_lowering=False)
    TQS = TS // 2

    def din(name, shape, dt=F32):
        return nc.dram_tensor(name, shape, dt, kind="ExternalInput")

    xp_t = din("xp", [TP, D])
    xs_t = din("xs", [TS, D])
    w_in_t = din("w_in", [D, DIN])
    w_out_t = din("w_out", [D, D])
    norm_g_t = din("norm_g", [1, D])
    final_g_t = din("final_g", [1, D])
    sink_t = din("sink", [1, 16])
    lq1_t = din("lq1", [1, 64])
    lk1_t = din("lk1", [1, 64])
    lq2_t = din("lq2", [1, 64])
    lk2_t = din("lk2", [1, 64])
    subln_t = din("subln_g", [1, 128])
    rb_t = din("rb", [2, 32, 24])
    ident_t = din("ident", [128, 128], BF16)
    E_t = din("E", [32, 512])
    mask_t = din("mask", [128, 384])
    yp_t = nc.dram_tensor("yp", [TP, D], F32, kind="ExternalOutput")
    ys_t = nc.dram_tensor("ys", [TQS, D], F32, kind="ExternalOutput")

    Wb_t = nc.dram_tensor("Wb", [D, DIN], BF16)
    WOb_t = nc.dram_tensor("WOb", [D, D], BF16)
    Zd_t = nc.dram_tensor("Zd", [48, 128, 512], F32)
    dk = "ExternalOutput" if debug else "Internal"
    BTA_t = nc.dram_tensor("BTA", [2, 16, 128, 384], F32, kind=dk)
    BTB_t = nc.dram_tensor("BTB", [2, 8, 128, 2, 384], F32, kind=dk)

    jobs = []
    for ji, (x_t, y_t, T, TQ) in enumerate([(xp_t, yp_t, TP, TP), (xs_t, ys_t, TS, TQS)]):
        j = dict(idx=ji, x=x_t.ap(), y=y_t.ap(), T=T, TQ=TQ)
        j["QAT"] = nc.dram_tensor("QAT%d" % ji, [1024, TQ], BF16, kind=dk).ap()
        j["KAT"] = nc.dram_tensor("KAT%d" % ji, [256, T], BF16, kind=dk).ap()
        j["GAT"] = nc.dram_tensor("GAT%d" % ji, [1024, TQ], BF16, kind=dk).ap()
        j["QBT"] = nc.dram_tensor("QBT%d" % ji, [1024, TQ], BF16, kind=dk).ap()
        j["KBT"] = nc.dram_tensor("KBT%d" % ji, [1024, T], BF16, kind=dk).ap()
        j["GBT"] = nc.dram_tensor("GBT%d" % ji, [1024, TQ], BF16, kind=dk).ap()
        j["VA"] = nc.dram_tensor("VA%d" % ji, [T, 256], BF16, kind=dk).ap()
        j["VB"] = nc.dram_tensor("VB%d" % ji, [T, 1024], BF16, kind=dk).ap()
        j["MIXT"] = nc.dram_tensor("MIXT%d" % ji, [2048, TQ], BF16, kind=dk).ap()
        jobs.append(j)

    with ExitStack() as stack:
        K = KB(nc, stack)
        pe, act, dve, pool, sp = nc.tensor, nc.scalar, nc.vector, nc.gpsimd, nc.sync

        ident = K.sb("ident", [128, 128], BF16)
        gN = K.sb("gN", [128, D], F32)
        gF = K.sb("gF", [128, D], F32)
        ESK = K.sb("esk", [128, 16], F32)
        NEGLAM = K.sb("neglam", [128, 1], F32)
        SG = K.sb("sg", [128, 1], F32)
        FB = K.sb("fb", [128, 32], F32)
        ones_bf = K.sb("ones_bf", [128, 128], BF16)
        ones_f = K.sb("ones_f", [128, 128], F32)

        cast_ev = {}

        def bcast_rows(t, off, n):
            return bass.AP(t, off, [[0, 128], [1, n]])

        with ExitStack() as ps:
            K0 = K
            old_stack = K.stack
            K.stack = ps
            st_c = K.stream("cload")

            lam_t = [K.sb("lam%d" % i, [128, 64], F32) for i in range(4)]
            RB = K.sb("RB", [32, 48], F32)
            RBrep = K.sb("RBrep", [32, 48, 128], F32)
            Es = K.sb("Es", [32, 512], F32)
            MASK = K.sb("MASK", [128, 384], F32)
            sgt = K.sb("sgt", [128, 1], F32)
            lsum = K.sb("lsum", [128, 2], F32)
            lprod = K.sb("lprod", [128, 64], F32)
            lexp = K.sb("lexp", [128, 2], F32)

            K.sig(sp.dma_start(out=ident[:], in_=ident_t.ap()), st_c, 16)
            K.sig(sp.dma_start(out=gN[:], in_=bcast_rows(norm_g_t, 0, D)), st_c, 16)
            K.sig(sp.dma_start(out=gF[:], in_=bcast_rows(final_g_t, 0, D)), st_c, 16)
            K.sig(sp.dma_start(out=ESK[:], in_=bcast_rows(sink_t, 0, 16)), st_c, 16)
            for i, t in enumerate([lq1_t, lk1_t, lq2_t, lk2_t]):
                K.sig(sp.dma_start(out=lam_t[i][:], in_=bcast_rows(t, 0, 64)), st_c, 16)
            K.sig(sp.dma_start(out=sgt[:], in_=bass.AP(subln_t, 0, [[1, 128], [1, 1]])), st_c, 16)
            for tab in range(2):
                K.sig(sp.dma_start(out=RB[:, tab * 24:(tab + 1) * 24], in_=rb_t.ap()[tab]), st_c, 16)
            K.sig(sp.dma_start(out=Es[:], in_=E_t.ap()), st_c, 16)
            K.sig(sp.dma_start(out=MASK[:], in_=mask_t.ap()), st_c, 16)
            cl = (st_c, st_c.n)

            st_d = K.stream("p_dve")
            st_a = K.stream("p_act")
            st_p = K.stream("p_pe")
            K.wait("dve", cl)
            K.wait("act", cl)
            K.wait("pe", cl)
            dve.memset(ones_bf[:], 1.0)
            dve.memset(ones_f[:], 1.0)
            ev = K.sig(act.activation(out=ESK[:], in_=ESK[:], func=AF.Exp), st_a)
            e = K.sig(dve.tensor_tensor(out=lprod[:], in0=lam_t[0][:], in1=lam_t[1][:], op=ALU.mult), st_d)
            K.wait("dve", e)
            e = K.sig(dve.reduce_sum(out=lsum[:, 0:1], in_=lprod[:], axis=AX.X), st_d)
            K.wait("dve", e)
            e = K.sig(dve.tensor_tensor(out=lprod[:], in0=lam_t[2][:], in1=lam_t[3][:], op=ALU.mult), st_d)
            K.wait("dve", e)
            e = K.sig(dve.reduce_sum(out=lsum[:, 1:2], in_=lprod[:], axis=AX.X), st_d)
            K.wait("act", e)
            e = K.sig(act.activation(out=lexp[:], in_=lsum[:], func=AF.Exp), st_a)
            K.wait("dve", e)
            e = K.sig(dve.tensor_tensor(out=NEGLAM[:], in0=lexp[:, 1:2], in1=lexp[:, 0:1], op=ALU.subtract), st_d)
            K.wait("dve", e)
            e = K.sig(dve.tensor_scalar(out=NEGLAM[:], in0=NEGLAM[:], scalar1=-LAM_INIT, scalar2=None, op0=ALU.add), st_d)
            e = K.sig(dve.tensor_scalar(out=SG[:], in0=sgt[:], scalar1=(1.0 - LAM_INIT), scalar2=None, op0=ALU.mult), st_d)
            e = K.sig(dve.tensor_copy(out=RBrep[:], in_=bass.AP(RB, 0, [[48, 32], [1, 48], [0, 128]])), st_d)
            rbrep_ready = e

            NS = 4
            Zs = [K.sb("Zs", [128, 512], F32) for i in range(NS)]
            BTs = [K.sb("BTs", [128, 384], F32) for i in range(NS)]
            BTo = [K.sb("BTo", [128, 2, 384], F32) for i in range(NS)]
            zp = [K.psum("zp", [128, 512], F32) for i in range(2)]
            st_z = [K.stream("zst%d" % i) for i in range(NS)]
            st_b = [K.stream("bld%d" % i) for i in range(NS)]
            st_o = [K.stream("bst%d" % i) for i in range(NS)]
            zp_free = [None, None]
            zs_free = [None] * NS
            bts_free = [None] * NS
            bto_free = [None] * NS
            pst = {}
            K.wait("pe", rbrep_ready)

            def stage_a(idx):
                tab, h = idx // 24, idx % 24
                s, z = idx % NS, idx % 2
                K.wait("pe", zp_free[z])
                e_mm = K.sig(pe.matmul(zp[z][:], lhsT=RBrep[:, idx, :], rhs=Es[:], start=True, stop=True), st_p)
                K.wait("dve", e_mm)
                K.wait("dve", zs_free[s])
                e_z = K.sig(dve.tensor_copy(out=Zs[s][:], in_=zp[z][:]), st_d)
                zp_free[z] = e_z
                e_z2 = None
                if h >= 16:
                    K.wait("dve", e_z)
                    fo = (tab * 8 + (h - 16)) * 2
                    e_z1 = K.sig(dve.tensor_copy(out=FB[:, fo:fo + 1], in_=Zs[s][:, 510:511]), st_d)
                    e_z2 = K.sig(dve.tensor_copy(out=FB[:, fo + 1:fo + 2], in_=Zs[s][:, 0:1]), st_d)
                    e_z2 = [e_z1, e_z2]
                K.wait("sp", e_z)
                e_st = K.sig(sp.dma_start(out=Zd_t.ap()[idx], in_=Zs[s][:]), st_z[s], 16)
                zs_free[s] = e_st
                pst[idx] = (e_st, e_z2)

            def stage_b(idx):
                s = idx % NS
                e_st, e_z2 = pst[idx]
                K.wait("sp", e_st)
                K.wait("sp", bts_free[s])
                e_ld = K.sig(sp.dma_start(out=BTs[s][:],
                                          in_=bass.AP(Zd_t, idx * 128 * 512 + 127, [[511, 128], [1, 384]])),
                             st_b[s], 16)
                pst[idx] = (e_ld, e_z2)

            def stage_c(idx):
                tab, h = idx // 24, idx % 24
                s = idx % NS
                e_ld, e_z2 = pst.pop(idx)
                K.wait("dve", e_ld)
                K.wait("dve", bto_free[s])
                if h < 16:
                    e_o = K.sig(dve.tensor_tensor(out=BTo[s][:, 0, :], in0=BTs[s][:], in1=MASK[:], op=ALU.add), st_d)
                    bts_free[s] = e_o
                    K.wait("sp", e_o)
                    e_os = K.sig(sp.dma_start(out=BTA_t.ap()[tab, h], in_=BTo[s][:, 0, :]), st_o[s], 16)
                else:
                    K.wait("dve", e_z2)
                    fo = (tab * 8 + (h - 16)) * 2
                    e_o1 = K.sig(dve.tensor_scalar(out=BTo[s][:, 0, :], in0=BTs[s][:], scalar1=FB[:, fo:fo + 1], scalar2=8.0,
                                                   op0=ALU.subtract, op1=ALU.mult), st_d)
                    e_o = K.sig(dve.tensor_scalar(out=BTo[s][:, 1, :], in0=BTs[s][:], scalar1=FB[:, fo + 1:fo + 2],
                                                  scalar2=8.0, op0=ALU.subtract, op1=ALU.mult), st_d)
                    bts_free[s] = [e_o1, e_o]
                    K.wait("sp", e_o1)
                    K.wait("sp", e_o)
                    e_os = K.sig(sp.dma_start(out=BTB_t.ap()[tab, h - 16], in_=BTo[s][:]), st_o[s], 16)
                bto_free[s] = e_os
                return e_os

            last_os = None
            for it in range(48 + 2):
                if it < 48:
                    stage_a(it)
                if 0 <= it - 1 < 48:
                    stage_b(it - 1)
                if 0 <= it - 2 < 48:
                    last_os = stage_c(it - 2)
            K.wait("pool", last_os)
            for i in range(NS):
                K.wait("pool", (st_o[i], st_o[i].n))
            for blk in range(13):
                stc = K.stream("cast%d" % blk)
                stc.nobar = True
                cast_ev[blk] = K.sig(pool.dma_start(out=Wb_t.ap()[:, blk * 512:(blk + 1) * 512],
                                                    in_=w_in_t.ap()[:, blk * 512:(blk + 1) * 512]), stc, 16)
            stc = K.stream("castwo")
            stc.nobar = True
            cast_ev["wo"] = K.sig(pool.dma_start(out=WOb_t.ap(), in_=w_out_t.ap()), stc, 16)
            K.barrier()
            K.stack = old_stack

        def gemm1(job):
            T, TQ = job["T"], job["TQ"]
            x = job["x"]
            NG = T // 512
            Wv = Wb_t.ap().rearrange("(kc p) c -> p kc c", p=128)
            with ExitStack() as es:
                old = K.stack
                K.stack = es
                xt = [K.sb("xt", [128, D], F32) for _ in range(2)]
                junk = K.sb("junk", [128, D], BF16)
                xn = [K.sb("xn", [128, D], BF16) for _ in range(2)]
                xnT = [K.sb("xnT", [128, 16, 512], BF16) for _ in range(2)]
                wb = [K.sb("wb", [128, 16, 512], BF16) for _ in range(3)]
                NSTG = 4
                stg = [K.sb("stg", [128, 512], BF16) for _ in range(NSTG)]
                ssq = [K.sb("ssq", [128, 1], F32) for _ in range(2)]
                rstd = [K.sb("rstd", [128, 1], F32) for _ in range(2)]
                tp = [K.psum("tp", [128, 1024], BF16) for i in range(2)]
                NACC = 6
                acc = [K.psum("acc", [128, 512], F32) for i in range(NACC)]

                st_x = [K.stream("g1x%d" % i) for i in range(2)]
                st_w = [K.stream("g1w%d" % i) for i in range(3)]
                st_s = [K.stream("g1s%d" % i) for i in range(NSTG)]
                st_pe = K.stream("g1pe")
                st_act = K.stream("g1act")
                st_dve = K.stream("g1dve")

                xt_free = [[], []]
                ssq_free = [None, None]
                rstd_free = [None, None]
                xn_free = [None, None]
                tp_free = [None, None]
                xnT_ready = [[], []]
                xnT_free = [None, None]
                wb_free = [None] * 3
                acc_free = [None] * NACC
                stg_free = [None] * NSTG
                cnt = dict(acc=0, stg=0, wblk=0)
                fe_state = {}

                def fe_part1(tg, tt):
                    ti = tg * 4 + tt
                    s = ti % 2
                    K.wait("sp", xt_free[s])
                    e_ld = K.sig(sp.dma_start(out=xt[s][:], in_=x[ti * 128:(ti + 1) * 128, :]), st_x[s], 16)
                    K.wait("act", e_ld)
                    K.wait("act", ssq_free[s])
                    e_sq = K.sig(act.activation(out=junk[:], in_=xt[s][:], func=AF.Square, accum_out=ssq[s][:]), st_act)
                    K.wait("act", e_sq)
                    K.wait("act", rstd_free[s])
                    e1 = K.sig(act.activation(out=rstd[s][:], in_=ssq[s][:], func=AF.Sqrt, bias=EPS, scale=1.0 / D), st_act)
                    ssq_free[s] = e1
                    K.wait("dve", e1)
                    e2 = K.sig(dve.reciprocal(out=rstd[s][:], in_=rstd[s][:]), st_dve)
                    K.wait("dve", e2)
                    K.wait("dve", xn_free[s])
                    K.wait("dve", e_ld)
                    e3 = K.sig(dve.scalar_tensor_tensor(out=xn[s][:], in0=xt[s][:], scalar=rstd[s][:], in1=gN[:],
                                                        op0=ALU.mult, op1=ALU.mult), st_dve)
                    xt_free[s] = [e_sq, e3]
                    rstd_free[s] = e3
                    fe_state[ti] = e3

                def fe_part2(tg, tt):
                    ti = tg * 4 + tt
                    s = ti % 2
                    buf = tg % 2
                    K.wait("pe", fe_state.pop(ti))
                    evs = []
                    for b in range(2):
                        K.wait("pe", tp_free[b])
                        for jj in range(8):
                            kc = b * 8 + jj
                            ins = pe.transpose(tp[b][:, jj * 128:(jj + 1) * 128], xn[s][:, kc * 128:(kc + 1) * 128],
                                               ident[:])
                        e_t = K.sig(ins, st_pe)
                        if b == 1:
                            xn_free[s] = e_t
                        dst = xnT[buf][:, b * 8:(b + 1) * 8, tt * 128:(tt + 1) * 128]
                        src = tp[b][:].rearrange("p (j q) -> p j q", j=8)
                        if b == 0:
                            K.wait("act", e_t)
                            if tt == 0:
                                K.wait("act", xnT_free[buf])
                            e_c = K.sig(act.copy(out=dst, in_=src), st_act)
                        else:
                            K.wait("dve", e_t)
                            if tt == 0:
                                K.wait("dve", xnT_free[buf])
                            e_c = K.sig(dve.tensor_copy(out=dst, in_=src), st_dve)
                        tp_free[b] = e_c
                        evs.append(e_c)
                    if tt == 0:
                        xnT_ready[buf] = []
                    xnT_ready[buf] += evs

                def evac(a, kind, dst_ap, ncols=512, nparts=128):
                    s = cnt["stg"] % NSTG
                    cnt["stg"] += 1
                    eng = "act" if kind == "gate" else "dve"
                    K.wait(eng, acc_ready_ev[a])
                    K.wait(eng, stg_free[s])
                    if kind == "gate":
                        e = K.sig(act.activation(out=stg[s][:, 0:ncols], in_=acc[a][:, 0:ncols], func=AF.Silu), st_act)
                    else:
                        e = K.sig(dve.tensor_copy(out=stg[s][:, 0:ncols], in_=acc[a][:, 0:ncols]), st_dve)
                    acc_free[a] = e
                    K.wait("pool", e)
                    stg_free[s] = K.sig(pool.dma_start(out=dst_ap, in_=stg[s][:, 0:ncols]), st_s[s], 16)

                acc_ready_ev = [None] * NACC

                def blocks_for(full):
                    bl = []
                    if full:
                        bl.append((0, [("fm", "QAT", 0, c, "copy") for c in range(4)]))
                        bl.append((512, [("fm", "QAT", 512, c, "copy") for c in range(4)]))
                    bl.append((1024, [("fm", "KAT", 0, 0, "copy"), ("fm", "KAT", 0, 1, "copy"), ("tm", "VA", 0, 256, 256)]))
                    if full:
                        bl.append((1536, [("fm", "GAT", 0, c, "gate") for c in range(4)]))
                        bl.append((2048, [("fm", "GAT", 512, c, "gate") for c in range(4)]))
                        bl.append((2560, [("fm", "QBT", 0, c, "copy") for c in range(4)]))
                        bl.append((3072, [("fm", "QBT", 512, c, "copy") for c in range(4)]))
                    bl.append((3584, [("fm", "KBT", 0, c, "copy") for c in range(4)]))
                    bl.append((4096, [("fm", "KBT", 512, c, "copy") for c in range(4)]))
                    bl.append((4608, [("tm", "VB", 0, 0, 512)]))
                    bl.append((5120, [("tm", "VB", 512, 0, 512)]))
                    if full:
                        bl.append((5632, [("fm", "GBT", 0, c, "gate") for c in range(4)]))
                        bl.append((6144, [("fm", "GBT", 512, c, "gate") for c in range(4)]))
                    return bl

                def do_block(tg, col0, items):
                    buf = tg % 2
                    tok0 = tg * 512
                    ws = cnt["wblk"] % 3
                    cnt["wblk"] += 1
                    K.wait("sp", wb_free[ws])
                    K.wait("sp", cast_ev[col0 // 512])
                    e_w = K.sig(sp.dma_start(out=wb[ws][:], in_=Wv[:, :, col0:col0 + 512]), st_w[ws], 16)
                    K.wait("pe", e_w)
                    K.wait("pe", xnT_ready[buf])
                    last = None
                    for it in items:
                        if it[0] == "fm":
                            _, dname, rbase, c, kind = it
                            a = cnt["acc"] % NACC
                            cnt["acc"] += 1
                            K.wait("pe", acc_free[a])
                            for kc in range(16):
                                ins = pe.matmul(acc[a][:], lhsT=wb[ws][:, kc, c * 128:(c + 1) * 128],
                                                rhs=xnT[buf][:, kc, :], start=(kc == 0), stop=(kc == 15))
                            last = K.sig(ins, st_pe)
                            acc_ready_ev[a] = last
                            r0 = rbase + c * 128
                            evac(a, kind, job[dname][r0:r0 + 128, tok0:tok0 + 512])
                        else:
                            _, dname, dcol0, wc0, ncols = it
                            for tt in range(4):
                                a = cnt["acc"] % NACC
                                cnt["acc"] += 1
                                K.wait("pe", acc_free[a])
                                for kc in range(16):
                                    ins = pe.matmul(acc[a][:, 0:ncols], lhsT=xnT[buf][:, kc, tt * 128:(tt + 1) * 128],
                                                    rhs=wb[ws][:, kc, wc0:wc0 + ncols], start=(kc == 0), stop=(kc == 15))
                                last = K.sig(ins, st_pe)
                                acc_ready_ev[a] = last
                                t0 = tok0 + tt * 128
                                evac(a, "copy", job[dname][t0:t0 + 128, dcol0:dcol0 + ncols], ncols=ncols)
                    wb_free[ws] = last
                    return last

                for tt in range(4):
                    fe_part1(0, tt)
                    fe_part2(0, tt)
                for tg in range(NG):
                    full = (tg * 512) < TQ
                    bl = blocks_for(full)
                    last = None
                    for bi, (col0, items) in enumerate(bl):
                        last = do_block(tg, col0, items)
                        if tg + 1 < NG:
                            if bi < 4:
                                fe_part1(tg + 1, bi)
                            if 1 <= bi < 5:
                                fe_part2(tg + 1, bi - 1)
                    xnT_free[tg % 2] = last
                K.barrier()
                K.stack = old

        def attn_window(job, tab):
            T, TQ = job["T"], job["TQ"]
            NKB = T // 128
            NQB = TQ // 128
            QAv = job["QAT"].rearrange("(h d) t -> d h t", d=64)
            GAv = job["GAT"].rearrange("(h d) t -> d h t", d=64)
            MIXv = job["MIXT"][0:1024, :].rearrange("(h d) t -> d h t", d=64)
            VAv = job["VA"].rearrange("(kb p) c -> p kb c", p=128)
            with ExitStack() as es:
                old = K.stack
                K.stack = es
                HQ = TQ // 2
                NQH = NQB // 2
                KAw = [K.sb("KAw", [64, T], BF16) for _ in range(2)]
                VAw = [K.sb("VAw", [128, NKB, 64], BF16) for _ in range(2)]
                BAw = [K.sb("BAw", [128, 4, 384], F32) for _ in range(2)]
                QAw = [K.sb("QAw", [64, 4, HQ], BF16) for _ in range(2)]
                GAw = [K.sb("GAw", [64, 4, HQ], BF16) for _ in range(2)]
                NPA = 8
                PA = [K.sb("PA", [128, 512], BF16) for _ in range(NPA)]
                DENs = [K.sb("DEN", [64, 512], F32) for _ in range(2)]
                RRs = [K.sb("RR", [64, 512], F32) for _ in range(2)]
                OOs = [K.sb("OO", [64, 512], F32) for _ in range(2)]
                MSA = [K.sb("MSA", [64, 512], BF16) for _ in range(2)]
                NSA = 4
                SA = [K.psum("SA", [128, 512], F32) for i in range(NSA)]
                OAs = [K.psum("OA", [64, 512], F32) for i in range(2)]
                LAs = [K.psum("LA", [64, 512], F32) for i in range(2)]
                st_pool = K.stream("a1pool")
                st_kv = [K.stream("a1kv%d" % i) for i in range(2)]
                st_qg = [K.stream("a1qg%d" % i) for i in range(2)]
                st_ms = [K.stream("a1ms%d" % i) for i in range(2)]
                st_pe = K.stream("a1pe")
                st_act = K.stream("a1act")
                st_dve = K.stream("a1dve")
                sa_free = [None] * NSA
                pa_free = [None] * NPA
                ms_free = [None, None]
                cnt = dict(sa=0, ms=0)
                oa_free = [None, None]
                den_free = [None, None]
                rr_free = [None, None]
                oo_free = [None, None]
                cnt["it"] = 0
                cnt["pa"] = 0
                kv_free = [[], []]
                qg_free = [[], []]
                kv_ev = {}
                qg_ev = {}
                lastev = {}

                def load_kv(j):
                    hs = j % 2
                    K.wait("sp", kv_free[hs])
                    K.sig(sp.dma_start(out=KAw[hs][:], in_=job["KAT"][j * 64:(j + 1) * 64, :]), st_kv[hs], 16)
                    K.sig(sp.dma_start(out=VAw[hs][:], in_=VAv[:, :, j * 64:(j + 1) * 64]), st_kv[hs], 16)
                    for g in range(4):
                        e = K.sig(sp.dma_start(out=BAw[hs][:, g, :], in_=BTA_t.ap()[tab, 4 * j + g]), st_kv[hs], 16)
                    kv_ev[j] = e

                def load_qg(gi):
                    j, c = gi // 2, gi % 2
                    b = gi % 2
                    K.wait("sp", qg_free[b])
                    K.sig(sp.dma_start(out=QAw[b][:], in_=QAv[:, 4 * j:4 * j + 4, c * HQ:(c + 1) * HQ]), st_qg[b], 16)
                    qg_ev[gi] = K.sig(sp.dma_start(out=GAw[b][:], in_=GAv[:, 4 * j:4 * j + 4, c * HQ:(c + 1) * HQ]),
                                      st_qg[b], 16)

                load_kv(0)
                load_qg(0)
                for j in range(4):
                    hs = j % 2
                    load_qg(2 * j + 1)
                    if j + 1 < 4:
                        load_kv(j + 1)
                    K.wait("pe", kv_ev[j])
                    K.wait("dve", kv_ev[j])
                    esk_b = bass.AP(ESK, 4 * j, [[16, 64], [1, 4], [0, 128]])
                    pend = None
                    pend2 = None
                    for n in range(NQB + 2):
                        cur = None
                        if n < NQB:
                            kbs = [kb for kb in (n - 1, n, n + 1) if 0 <= kb < NKB]
                            pas = []
                            for kb in kbs:
                                delta = kb - n
                                s = cnt["sa"] % NSA
                                cnt["sa"] += 1
                                K.wait("pe", sa_free[s])
                                gi = 2 * j + n // NQH
                                nl = n % NQH
                                K.wait("pe", qg_ev[gi])
                                e_qk = K.sig(pe.matmul(SA[s][:], lhsT=KAw[hs][0:64, kb * 128:(kb + 1) * 128],
                                                       rhs=QAw[gi % 2][0:64, :, nl * 128:(nl + 1) * 128], start=True, stop=True),
                                             st_pe)
                                lastev[("qk", gi)] = e_qk
                                K.wait("dve", e_qk)
                                c0 = (1 - delta) * 128
                                e_b = K.sig(dve.scalar_tensor_tensor(
                                    out=SA[s][:].rearrange("p (g q) -> p g q", g=4),
                                    in0=SA[s][:].rearrange("p (g q) -> p g q", g=4), scalar=0.125,
                                    in1=BAw[hs][:, :, c0:c0 + 128], op0=ALU.mult, op1=ALU.add), st_dve)
                                lastev[("bias", j)] = e_b
                                pa = cnt["pa"] % NPA
                                cnt["pa"] += 1
                                K.wait("act", e_b)
                                K.wait("act", pa_free[pa])
                                e_x = K.sig(act.activation(out=PA[pa][:], in_=SA[s][:], func=AF.Exp), st_act)
                                sa_free[s] = e_x
                                pas.append((pa, kb, e_x))
                            cur = (n, pas)
                        if pend2 is not None:
                            pn2, b2, e2, e_pv2 = pend2
                            OA, RR, OO = OAs[b2], RRs[b2], OOs[b2]
                            K.wait("dve", e2)
                            K.wait("dve", oo_free[b2])
                            e3 = K.sig(dve.tensor_tensor(out=OO[:], in0=OA[:], in1=RR[:], op=ALU.mult), st_dve)
                            oa_free[b2] = e3
                            rr_free[b2] = e3
                            m = cnt["ms"] % 2
                            cnt["ms"] += 1
                            K.wait("pool", e3)
                            K.wait("pool", ms_free[m])
                            gi2 = 2 * j + pn2 // NQH
                            nl2 = pn2 % NQH
                            K.wait("pool", qg_ev[gi2])
                            e4 = K.sig(pool.tensor_tensor(out=MSA[m][:].rearrange("p (g q) -> p g q", g=4),
                                                          in0=OO[:].rearrange("p (g q) -> p g q", g=4),
                                                          in1=GAw[gi2 % 2][:, :, nl2 * 128:(nl2 + 1) * 128], op=ALU.mult),
                                       st_pool)
                            lastev[("g", gi2)] = e4
                            if nl2 == NQH - 1:
                                qg_free[gi2 % 2] = [lastev[("qk", gi2)], e4]
                                if gi2 % 2 == 0 and gi2 + 2 < 8:
                                    load_qg(gi2 + 2)
                            oo_free[b2] = e4
                            K.wait("pool", e4)
                            ms_free[m] = K.sig(pool.dma_start(out=MIXv[:, 4 * j:4 * j + 4, pn2 * 128:(pn2 + 1) * 128],
                                                              in_=MSA[m][:].rearrange("p (g q) -> p g q", g=4)),
                                               st_ms[m], 16)
                            lastev[("pv", j)] = e_pv2
                            pend2 = None
                        if pend is not None:
                            pn, pas = pend
                            b2 = cnt["it"] % 2
                            cnt["it"] += 1
                            OA, LA, DEN, RR = OAs[b2], LAs[b2], DENs[b2], RRs[b2]
                            K.wait("pe", oa_free[b2])
                            for i, (s, kb, e_x) in enumerate(pas):
                                K.wait("pe", e_x)
                                pe.matmul(OA[:], lhsT=VAw[hs][:, kb, :], rhs=PA[s][:], start=(i == 0), stop=(i == len(pas) - 1))
                            for i, (s, kb, e_x) in enumerate(pas):
                                ins = pe.matmul(LA[:], lhsT=ones_bf[:, 0:64], rhs=PA[s][:], start=(i == 0),
                                                stop=(i == len(pas) - 1))
                            e_pv = K.sig(ins, st_pe)
                            for (s, kb, e_x) in pas:
                                pa_free[s] = e_pv
                            K.wait("dve", e_pv)
                            K.wait("dve", den_free[b2])
                            e1 = K.sig(dve.tensor_tensor(out=DEN[:].rearrange("p (g q) -> p g q", g=4),
                                                         in0=LA[:].rearrange("p (g q) -> p g q", g=4),
                                                         in1=esk_b, op=ALU.add), st_dve)
                            K.wait("act", e1)
                            K.wait("act", rr_free[b2])
                            e2 = K.sig(act.activation(out=DEN[:], in_=DEN[:], func=AF.Ln), st_act)
                            K.wait("act", e2)
                            e2 = K.sig(act.activation(out=RR[:], in_=DEN[:], func=AF.Exp, scale=-1.0), st_act)
                            den_free[b2] = e2
                            pend2 = (pn, b2, e2, e_pv)
                        pend = cur
                    kv_free[hs] = [lastev[("pv", j)], lastev[("bias", j)]]
                K.barrier()
                K.stack = old

        def attn_diff(job, tab):
            T, TQ = job["T"], job["TQ"]
            NKB = T // 128
            NG = TQ // 512
            VBv = job["VB"].rearrange("(kb p) c -> p kb c", p=128)
            with ExitStack() as es:
                old = K.stack
                K.stack = es
                KT = [K.sb("KT", [128, T], BF16) for _ in range(2)]
                VV = [K.sb("VV", [128, NKB, 128], BF16) for _ in range(2)]
                QT = [K.sb("QT", [128, TQ], BF16) for _ in range(2)]
                GT = [K.sb("GT", [128, TQ], BF16) for _ in range(2)]
                BP = [K.sb("BP", [128, 2, 384], F32) for _ in range(2)]
                NPT = 6
                PT = [K.sb("PT", [128, 1024], BF16) for _ in range(NPT)]
                OS = K.sb("OS", [128, 1024], F32)
                LS = K.sb("LS", [128, 1024], F32)
                LN = K.sb("LN", [128, 1024], F32)
                RR = K.sb("RR", [128, 1024], F32)
                Asb = K.sb("Asb", [128, 512], F32)
                Bsb = K.sb("Bsb", [128, 512], F32)
                Osb = K.sb("Osb", [128, 512], F32)
                SQ = K.sb("SQ", [128, 512], F32)
                RS = K.sb("RS", [128, 512], F32)
                RS2 = K.sb("RS2", [128, 512], F32)
                TT = K.sb("TT", [128, 512], F32)
                MS = [K.sb("MS", [128, 512], BF16) for _ in range(2)]
                ST = [K.psum("ST", [128, 1024], F32) for i in range(2)]
                OO = K.psum("OO", [128, 1024], F32)
                LL = K.psum("LL", [128, 1024], F32)
                st_ld = [K.stream("a2ld%d" % i) for i in range(2)]
                st_ms = [K.stream("a2ms%d" % i) for i in range(2)]
                st_pe = K.stream("a2pe")
                st_act = K.stream("a2act")
                st_dve = K.stream("a2dve")
                st_pool = K.stream("a2pool")

                slot_free = [[], []]
                st_cons = [None, None]
                pt_free = [None] * NPT
                ms_free = [None, None]
                cnt = dict(pt=0, ms=0, user=0)
                state = {}

                def load_head(h):
                    hs = h % 2
                    K.wait("sp", slot_free[hs])
                    K.sig(sp.dma_start(out=KT[hs][:], in_=job["KBT"][h * 128:(h + 1) * 128, :]), st_ld[hs], 16)
                    K.sig(sp.dma_start(out=QT[hs][:], in_=job["QBT"][h * 128:(h + 1) * 128, :]), st_ld[hs], 16)
                    K.sig(sp.dma_start(out=GT[hs][:], in_=job["GBT"][h * 128:(h + 1) * 128, :]), st_ld[hs], 16)
                    K.sig(sp.dma_start(out=BP[hs][:], in_=BTB_t.ap()[tab, h]), st_ld[hs], 16)
                    return K.sig(sp.dma_start(out=VV[hs][:], in_=VBv[:, :, h * 128:(h + 1) * 128]), st_ld[hs], 16)

                ld_ev = {0: load_head(0)}
                if True:
                    ld_ev[1] = load_head(1)
                    users = []
                    LB_AT = min(5, NKB - 2)
                    SS_AT = min(14, NKB - 1)
                    prev = None
                    for h in range(8):
                        for g in range(NG):
                            for kb in range(NKB):
                                users.append(("qk", (h, g), kb))
                                if prev is not None and kb == LB_AT:
                                    users.append(("lb", prev, None))
                                if prev is not None and kb == SS_AT:
                                    users.append(("ss", prev, None))
                            prev = (h, g)
                    users.append(("lb", prev, None))
                    users.append(("ss", prev, None))
                    info = {}
                    info2 = {}
                    acc_done = {}
                    sq_ready = {}

                    def deps_exist(ui):
                        kind, hg, kb = users[ui]
                        if kind == "lb":
                            return hg in acc_done
                        if kind == "ss":
                            return hg in sq_ready
                        return True

                    def produce(ui):
                        kind, hg, kb = users[ui]
                        h, g = hg
                        hs = h % 2
                        s = cnt["user"] % 2
                        cnt["user"] += 1
                        K.wait("pe", st_cons[s])
                        if kind == "qk":
                            K.wait("pe", ld_ev[h])
                            pe.matmul(ST[s][:, 0:512], lhsT=KT[hs][0:64, kb * 128:(kb + 1) * 128],
                                      rhs=QT[hs][0:64, g * 512:(g + 1) * 512], start=True, stop=True)
                            e = K.sig(pe.matmul(ST[s][:, 512:1024], lhsT=KT[hs][64:128, kb * 128:(kb + 1) * 128],
                                                rhs=QT[hs][64:128, g * 512:(g + 1) * 512], start=True, stop=True), st_pe)
                            near = [jq for jq in range(4) if abs(kb - (4 * g + jq)) <= 1]
                            if near:
                                side = 0 if kb <= 4 * g + 1 else 1
                                jl, jh = near[0], near[-1]
                                nn = jh - jl + 1
                                c0 = (1 - (kb - (4 * g + jl))) * 128
                                K.wait("dve", e)
                                K.wait("dve", ld_ev[h])
                                stv = ST[s][:].rearrange("p (m q) -> p m q", m=2)[:, :, jl * 128:(jh + 1) * 128]
                                bpv = bass.AP(BP[hs], side * 384 + c0, [[768, 128], [0, 2], [1, nn * 128]])
                                e = K.sig(dve.tensor_tensor(out=stv, in0=stv, in1=bpv, op=ALU.add), st_dve)
                        elif kind == "lb":
                            K.wait("pe", acc_done[hg])
                            e = K.sig(pe.matmul(ST[s][:, 512:1024], lhsT=ones_f[:], rhs=LS[:, 512:1024], start=True, stop=True),
                                      st_pe)
                            state["ls_pe"] = e
                        else:
                            K.wait("pe", sq_ready[hg])
                            e = K.sig(pe.matmul(ST[s][:, 0:512], lhsT=ones_f[:], rhs=SQ[:], start=True, stop=True), st_pe)
                        info[ui] = (s, e)

                    def consume(ui):
                        kind, hg, kb = users[ui]
                        h, g = hg
                        hs = h % 2
                        s, e_st = info.pop(ui)
                        if kind == "qk":
                            near = [jq for jq in range(4) if abs(kb - (4 * g + jq)) <= 1]
                            side = 0 if kb <= 4 * g + 1 else 1
                            if not near:
                                side = 1 if kb > 4 * g + 3 else 0
                            e_in = e_st
                            p = cnt["pt"] % NPT
                            cnt["pt"] += 1
                            K.wait("act", e_in)
                            K.wait("act", pt_free[p])
                            fo = (tab * 8 + h) * 2 + side
                            e_x = K.sig(act.activation(out=PT[p][:], in_=ST[s][:], func=AF.Exp,
                                                       bias=FB[:, fo:fo + 1], scale=0.125), st_act)
                            st_cons[s] = e_x
                            info2[ui] = (p, e_x)
                        elif kind == "lb":
                            K.wait("act", e_st)
                            K.wait("act", state.get("rr_free"))
                            e = K.sig(act.activation(out=LN[:, 512:1024], in_=ST[s][:, 512:1024], func=AF.Ln), st_act)
                            st_cons[s] = e
                            K.wait("act", acc_done[hg])
                            e = K.sig(act.activation(out=LN[:, 0:512], in_=LS[:, 0:512], func=AF.Ln), st_act)
                            state["ls_free"] = e
                            K.wait("act", e)
                            e = K.sig(act.activation(out=RR[:], in_=LN[:], func=AF.Exp, scale=-1.0), st_act)
                            K.wait("dve", e)
                            K.wait("dve", state.get("ab_free"))
                            K.wait("dve", state.get("os_ready"))
                            e = K.sig(dve.tensor_tensor(out=Asb[:], in0=OS[:, 0:512], in1=RR[:, 0:512], op=ALU.mult), st_dve)
                            e = K.sig(dve.scalar_tensor_tensor(out=Bsb[:], in0=OS[:, 512:1024], scalar=NEGLAM[:],
                                                               in1=RR[:, 512:1024], op0=ALU.mult, op1=ALU.mult), st_dve)
                            state["os_free"] = e
                            state["rr_free"] = e
                            K.wait("pool", e)
                            K.wait("pool", state.get("osb_free"))
                            e = K.sig(pool.tensor_tensor(out=Osb[:], in0=Asb[:], in1=Bsb[:], op=ALU.add), st_pool)
                            state["ab_free"] = e
                            K.wait("pool", e)
                            K.wait("pool", state.get("sq_free"))
                            e = K.sig(pool.tensor_tensor(out=SQ[:], in0=Osb[:], in1=Osb[:], op=ALU.mult), st_pool)
                            sq_ready[hg] = e
                        else:
                            state["sq_free"] = e_st
                            K.wait("act", e_st)
                            K.wait("act", state.get("rs_free"))
                            e = K.sig(act.activation(out=RS[:], in_=ST[s][:, 0:512], func=AF.Ln, bias=EPS, scale=1.0 / 128),
                                      st_act)
                            st_cons[s] = e
                            K.wait("act", e)
                            e = K.sig(act.activation(out=RS2[:], in_=RS[:], func=AF.Exp, scale=-0.5), st_act)
                            K.wait("pool", e)
                            K.wait("pool", state.get("tt_free"))
                            e = K.sig(pool.tensor_tensor(out=TT[:], in0=Osb[:], in1=RS2[:], op=ALU.mult), st_pool)
                            state["osb_free"] = e
                            state["rs_free"] = e
                            K.wait("dve", e)
                            m = cnt["ms"] % 2
                            cnt["ms"] += 1
                            K.wait("dve", ms_free[m])
                            K.wait("dve", ld_ev[h])
                            e = K.sig(dve.scalar_tensor_tensor(out=MS[m][:], in0=TT[:], scalar=SG[:],
                                                               in1=GT[hs][:, g * 512:(g + 1) * 512],
                                                               op0=ALU.mult, op1=ALU.mult), st_dve)
                            state["tt_free"] = e
                            K.wait("pool", e)
                            r0 = 1024 + h * 128
                            ms_free[m] = K.sig(pool.dma_start(out=job["MIXT"][r0:r0 + 128, g * 512:(g + 1) * 512],
                                                              in_=MS[m][:]), st_ms[m], 16)
                            state["last"] = e
                            if g == NG - 1 and h + 2 < 8:
                                slot_free[hs] = state[("hd", h)] + [e]
                                ld_ev[h + 2] = load_head(h + 2)


                    def consume_b(ui):
                        kind, hg, kb = users[ui]
                        h, g = hg
                        hs = h % 2
                        if kind != "qk":
                            return
                        p, e_x = info2.pop(ui)
                        if True:
                            first, last = (kb == 0), (kb == NKB - 1)
                            K.wait("pe", e_x)
                            if first:
                                K.wait("pe", state.get("oo_free"))
                            pe.matmul(OO[:, 0:512], lhsT=VV[hs][:, kb, :], rhs=PT[p][:, 0:512], start=first, stop=last)
                            pe.matmul(OO[:, 512:1024], lhsT=VV[hs][:, kb, :], rhs=PT[p][:, 512:1024], start=first, stop=last)
                            e_pv = K.sig(pe.matmul(LL[:, 0:512], lhsT=ones_bf[:], rhs=PT[p][:, 0:512], start=first, stop=last),
                                         st_pe)
                            K.wait("dve", e_x)
                            if first:
                                e_d = K.sig(dve.tensor_copy(out=LL[:, 512:1024], in_=PT[p][:, 512:1024]), st_dve)
                            else:
                                e_d = K.sig(dve.tensor_tensor(out=LL[:, 512:1024], in0=LL[:, 512:1024], in1=PT[p][:, 512:1024],
                                                              op=ALU.add), st_dve)
                            pt_free[p] = [e_pv, e_d]
                            if last:
                                K.wait("dve", e_pv)
                                K.wait("dve", state.get("os_free"))
                                K.wait("dve", state.get("ls_free"))
                                K.wait("dve", state.get("ls_pe"))
                                e_os = K.sig(dve.tensor_copy(out=OS[:], in_=OO[:]), st_dve)
                                state["os_ready"] = e_os
                                e = K.sig(dve.tensor_copy(out=LS[:, 0:512], in_=LL[:, 0:512]), st_dve)
                                state["oo_free"] = [e_os, e]
                                e = K.sig(dve.tensor_copy(out=LS[:, 512:1024], in_=LL[:, 512:1024]), st_dve)
                                acc_done[hg] = e
                                state[("hd", h)] = [e_pv, e_d]

                    NU = len(users)
                    nxt = 0
                    while nxt <= min(1, NU - 1) and deps_exist(nxt):
                        produce(nxt)
                        nxt += 1
                    deferred = None
                    for ui in range(NU):
                        assert nxt > ui
                        consume(ui)
                        if users[ui][0] == "qk":
                            while nxt <= min(ui + 2, NU - 1) and deps_exist(nxt):
                                produce(nxt)
                                nxt += 1
                        if deferred is not None:
                            consume_b(deferred)
                            deferred = None
                        if users[ui][0] == "qk" and users[ui][2] == 0 and ui + 1 < NU and users[ui + 1][0] == "qk":
                            deferred = ui
                        else:
                            consume_b(ui)
                        while nxt <= min(ui + 2, NU - 1) and deps_exist(nxt):
                            produce(nxt)
                            nxt += 1
                    assert deferred is None
                K.barrier()
                K.stack = old

        def gemm3(job):
            TQ = job["TQ"]
            x, y = job["x"], job["y"]
            NG = TQ // 512
            WOv = WOb_t.ap().rearrange("(kc p) c -> p kc c", p=128)
            MTv = job["MIXT"].rearrange("(kc p) t -> p kc t", p=128)
            with ExitStack() as es:
                old = K.stack
                K.stack = es
                WO = K.sb("WO", [128, 16, D], BF16)
                MT = [K.sb("MT", [128, 16, 512], BF16) for _ in range(2)]
                xt = [K.sb("xt3", [128, D], F32) for _ in range(2)]
                zt = [K.sb("zt", [128, D], F32) for _ in range(2)]
                ot = [K.sb("ot", [128, D], F32) for _ in range(2)]
                junk = K.sb("junk3", [128, D], BF16)
                ssq = [K.sb("ssq3", [128, 1], F32) for _ in range(2)]
                rstd = [K.sb("rstd3", [128, 1], F32) for _ in range(2)]
                Y = [K.psum("Y", [128, D], F32) for i in range(2)]
                st_wo = K.stream("g3wo")
                st_mt = [K.stream("g3mt%d" % i) for i in range(2)]
                st_x = [K.stream("g3x%d" % i) for i in range(2)]
                st_o = [K.stream("g3o%d" % i) for i in range(2)]
                st_pe = K.stream("g3pe")
                st_act = K.stream("g3act")
                st_dve = K.stream("g3dve")
                K.wait("sp", cast_ev["wo"])
                for q in range(4):
                    e_wo = K.sig(sp.dma_start(out=WO[:, q * 4:(q + 1) * 4, :], in_=WOv[:, q * 4:(q + 1) * 4, :]), st_wo, 16)
                K.wait("pe", e_wo)
                mt_free = [None, None]
                xt_free = [None, None]
                y_free = [None, None]
                zt_free = [None, None]
                ot_free = [None, None]
                ssq_free = [None, None]
                rstd_free = [None, None]
                mt_ev = {}

                def load_mt(tg):
                    b = tg % 2
                    K.wait("sp", mt_free[b])
                    mt_ev[tg] = K.sig(sp.dma_start(out=MT[b][:], in_=MTv[:, :, tg * 512:(tg + 1) * 512]), st_mt[b], 16)

                load_mt(0)
                for tg in range(NG):
                    b = tg % 2
                    if tg + 1 < NG:
                        load_mt(tg + 1)
                    K.wait("pe", mt_ev[tg])
                    for tt in range(4):
                        ti = tg * 4 + tt
                        s = ti % 2
                        K.wait("sp", xt_free[s])
                        e_x = K.sig(sp.dma_start(out=xt[s][:], in_=x[ti * 128:(ti + 1) * 128, :]), st_x[s], 16)
                        K.wait("pe", y_free[s])
                        for cb in range(4):
                            for kc in range(16):
                                ins = pe.matmul(Y[s][:, cb * 512:(cb + 1) * 512], lhsT=MT[b][:, kc, tt * 128:(tt + 1) * 128],
                                                rhs=WO[:, kc, cb * 512:(cb + 1) * 512], start=(kc == 0), stop=(kc == 15))
                        e_y = K.sig(ins, st_pe)
                        if tt == 3:
                            mt_free[b] = e_y
                        K.wait("dve", e_y)
                        K.wait("dve", e_x)
                        K.wait("dve", zt_free[s])
                        e_z = K.sig(dve.tensor_tensor(out=zt[s][:], in0=Y[s][:], in1=xt[s][:], op=ALU.add), st_dve)
                        y_free[s] = e_z
                        xt_free[s] = e_z
                        K.wait("act", e_z)
                        K.wait("act", ssq_free[s])
                        e_sq = K.sig(act.activation(out=junk[:], in_=zt[s][:], func=AF.Square, accum_out=ssq[s][:]), st_act)
                        K.wait("act", e_sq)
                        K.wait("act", rstd_free[s])
                        e1 = K.sig(act.activation(out=rstd[s][:], in_=ssq[s][:], func=AF.Sqrt, bias=EPS, scale=1.0 / D),
                                   st_act)
                        ssq_free[s] = e1
                        K.wait("dve", e1)
                        e2 = K.sig(dve.reciprocal(out=rstd[s][:], in_=rstd[s][:]), st_dve)
                        K.wait("dve", e2)
                        K.wait("dve", ot_free[s])
                        e3 = K.sig(dve.scalar_tensor_tensor(out=ot[s][:], in0=zt[s][:], scalar=rstd[s][:], in1=gF[:],
                                                            op0=ALU.mult, op1=ALU.mult), st_dve)
                        zt_free[s] = e3
                        rstd_free[s] = e3
                        K.wait("pool", e3)
                        ot_free[s] = K.sig(pool.dma_start(out=y[ti * 128:(ti + 1) * 128, :], in_=ot[s][:]), st_o[s], 16)
                K.barrier()
                K.stack = old

        for ji, job in enumerate(jobs):
            gemm1(job)
            attn_window(job, ji)
            attn_diff(job, ji)
            gemm3(job)
        K.barrier(final=True)
    return nc


def make_in_maps(inputs, TP, TS, n_cores=8):
    ident, E, mask = host_consts()
    rb = np.asarray(inputs["rel_bias"], np.float32)
    rb_sw = rb.copy()
    rb_sw[1:16] = rb[17:32]
    rb_sw[17:32] = rb[1:16]
    w_in = np.ascontiguousarray(np.asarray(inputs["w_in"], np.float32)[0])
    w_out = np.ascontiguousarray(np.asarray(inputs["w_out"], np.float32)[0])
    maps = []
    for c in range(n_cores):
        xp = np.ascontiguousarray(np.asarray(inputs["x_prompt"][c], np.float32)[:TP])
        xs = np.asarray(inputs["x_sample"][c // 2], np.float32)[:TS]
        if c % 2 == 1:
            xs = xs[::-1]
        xs = np.ascontiguousarray(xs)
        rbc = np.stack([rb, rb if c % 2 == 0 else rb_sw]).astype(np.float32)
        maps.append({
            "xp": xp, "xs": xs, "w_in": w_in, "w_out": w_out,
            "norm_g": np.asarray(inputs["norm_g"], np.float32).reshape(1, D),
            "final_g": np.asarray(inputs["final_g"], np.float32).reshape(1, D),
            "sink": np.asarray(inputs["sink"], np.float32).reshape(1, 16),
            "lq1": np.asarray(inputs["lambda_q1"], np.float32).reshape(1, 64),
            "lk1": np.asarray(inputs["lambda_k1"], np.float32).reshape(1, 64),
            "lq2": np.asarray(inputs["lambda_q2"], np.float32).reshape(1, 64),
            "lk2": np.asarray(inputs["lambda_k2"], np.float32).reshape(1, 64),
            "subln_g": np.asarray(inputs["subln_g"], np.float32).reshape(1, 128),
            "rb": rbc, "ident": ident, "E": E, "mask": mask,
        })
    return maps


def kernel(x_prompt, x_sample, norm_g, w_in, w_out, sink, lambda_q1, lambda_k1, lambda_q2, lambda_k2,
           subln_g, rel_bias, final_g):
    TP, TS = 4096, 8192
    inputs = dict(x_prompt=x_prompt, x_sample=x_sample, norm_g=norm_g, w_in=w_in, w_out=w_out, sink=sink,
                  lambda_q1=lambda_q1, lambda_k1=lambda_k1, lambda_q2=lambda_q2, lambda_k2=lambda_k2,
                  subln_g=subln_g, rel_bias=rel_bias, final_g=final_g)
    nc = build_program(TP, TS)
    maps = make_in_maps(inputs, TP, TS)
    res = run_bass_kernel_spmd(nc, maps, core_ids=list(range(8)))
    y_prompt = np.empty((8, TP, D), np.float32)
    y_sample = np.empty((4, TS, D), np.float32)
    for c in range(8):
        r = res.results[c]
        y_prompt[c] = r["yp"]
        if c % 2 == 0:
            y_sample[c // 2, :TS // 2] = r["ys"]
        else:
            y_sample[c // 2, TS // 2:] = r["ys"][::-1]
    return (y_prompt, y_sample)
```
